# Optimizing a Trainium2 kernel written in Bass

```python
import jax, jax.numpy as jnp
from jax import lax
import numpy as np

D_MODEL = 1024
BATCH = 8
SEQ = 8192
DEPTH = 4

D_A = D_MODEL
G_A = 8
GROUP_A = D_A // G_A
CHUNK_A = 128
H_B = 8
DK_B = D_MODEL // H_B
DV_B = D_MODEL // H_B
D_B = H_B * DK_B
CHUNK_B = 64
D_FF = 4 * D_MODEL
IN_WIDTHS = (D_A, D_A, D_B, D_B, D_B, D_B, D_MODEL, D_MODEL)
N_IN = 2 * D_A + 4 * D_B + 2 * D_MODEL
EPS = 1e-6

kernel_name = 'hybrid_sgu_hgrn2_block'


def rms_norm(x, g):
    xf = x.astype(jnp.float32)
    y = xf * lax.rsqrt(jnp.mean(xf * xf, axis=-1, keepdims=True) + EPS)
    return (y * g.astype(jnp.float32)).astype(x.dtype)


def layer_norm(x, g, b):
    xf = x.astype(jnp.float32)
    mu = jnp.mean(xf, axis=-1, keepdims=True)
    xc = xf - mu
    y = xc * lax.rsqrt(jnp.mean(xc * xc, axis=-1, keepdims=True) + EPS)
    return (y * g.astype(jnp.float32) + b.astype(jnp.float32)).astype(x.dtype)


def spatial_gating(u, v, ln_g, ln_b, w_s, b_s):
    B, S, _ = v.shape
    v = layer_norm(v, ln_g, ln_b)
    v = v.reshape(B, S // CHUNK_A, CHUNK_A, G_A, GROUP_A)
    causal = jnp.tril(jnp.ones((CHUNK_A, CHUNK_A), dtype=bool))
    w = jnp.where(causal[None], w_s, jnp.zeros_like(w_s))
    mixed = jnp.einsum('gts,bnsgc->bntgc', w, v) + jnp.transpose(b_s)[None, None, :, :, None]
    return u * mixed.reshape(B, S, D_A)


def hgrn2_chunk_step(state, inputs):
    qc, lfc, kc, vc = inputs
    bcum = jnp.cumsum(lfc, axis=-2)
    causal = jnp.tril(jnp.ones((CHUNK_B, CHUNK_B), dtype=bool))[:, :, None]
    diff = bcum[..., :, None, :] - bcum[..., None, :, :]
    decay = jnp.where(causal, jnp.exp(jnp.where(causal, diff, 0.0)), 0.0)
    scores = jnp.einsum('bhtk,bhsk,bhtsk->bhts', qc, kc, decay)
    o_intra = jnp.einsum('bhts,bhsv->bhtv', scores, vc)
    o_inter = jnp.einsum('bhtk,bhkv->bhtv', qc * jnp.exp(bcum), state)
    b_last = bcum[..., -1:, :]
    new_state = (jnp.exp(b_last[..., 0, :])[..., None] * state
                 + jnp.einsum('bhsk,bhsv->bhkv', kc * jnp.exp(b_last - bcum), vc))
    return new_state, o_intra + o_inter


def hgrn2_mix(q_raw, f_raw, i_raw, g_raw, lb, norm_g):
    B, S, _ = q_raw.shape
    nc = S // CHUNK_B
    q = jax.nn.silu(q_raw.astype(jnp.float32))
    lbf = lb.astype(jnp.float32)
    log_f = jnp.logaddexp(jnp.log(lbf), jnp.log1p(-lbf) + jax.nn.log_sigmoid(f_raw.astype(jnp.float32)))
    k = -jnp.expm1(log_f)
    v = i_raw.astype(jnp.float32)

    def to_chunks(t, dh):
        return t.reshape(B, nc, CHUNK_B, H_B, dh).transpose(1, 0, 3, 2, 4)

    xs = (to_chunks(q, DK_B), to_chunks(log_f, DK_B), to_chunks(k, DK_B), to_chunks(v, DV_B))
    s0 = jnp.zeros((B, H_B, DK_B, DV_B), jnp.float32)
    _, o = lax.scan(hgrn2_chunk_step, s0, xs)
    o = o.transpose(1, 0, 3, 2, 4).reshape(B, S, H_B, DV_B)
    gate = jax.nn.silu(g_raw.astype(jnp.float32)).reshape(B, S, H_B, DV_B)
    o = rms_norm(o, norm_g) * gate
    return o.reshape(B, S, D_B).astype(q_raw.dtype)


def setup_inputs(seed: int = 0) -> dict:
    key = jax.random.key(seed)
    ks = jax.random.split(key, 20)
    f32 = jnp.float32

    def nrm(k, shape, scale):
        return jax.random.normal(k, shape, f32) * scale

    return {
        'x': jax.random.normal(ks[0], (BATCH, SEQ, D_MODEL), f32),
        'mix_norm_g': 1.0 + nrm(ks[1], (DEPTH, D_MODEL), 0.02),
        'w_in': nrm(ks[2], (DEPTH, D_MODEL, N_IN), D_MODEL ** -0.5),
        'sgu_norm_g': 1.0 + nrm(ks[3], (DEPTH, D_A), 0.02),
        'sgu_norm_b': nrm(ks[4], (DEPTH, D_A), 0.02),
        'w_spatial': nrm(ks[5], (DEPTH, G_A, CHUNK_A, CHUNK_A), CHUNK_A ** -0.5),
        'b_spatial': 1.0 + nrm(ks[6], (DEPTH, G_A, CHUNK_A), 0.02),
        'lower_bounds': nrm(ks[7], (DEPTH, D_B), 0.1),
        'hgrn_norm_g': 1.0 + nrm(ks[8], (DEPTH, DV_B), 0.02),
        'w_branch_a': nrm(ks[9], (DEPTH, D_A, D_MODEL), D_A ** -0.5),
        'w_branch_b': nrm(ks[10], (DEPTH, D_B, D_MODEL), D_B ** -0.5),
        'w_out': nrm(ks[11], (DEPTH, D_MODEL, D_MODEL), D_MODEL ** -0.5),
        'mlp_norm_g': 1.0 + nrm(ks[12], (DEPTH, D_MODEL), 0.02),
        'w_mlp_up': nrm(ks[13], (DEPTH, D_MODEL, D_FF), D_MODEL ** -0.5),
        'w_mlp_down': nrm(ks[14], (DEPTH, D_FF, D_MODEL), D_FF ** -0.5),
        'final_norm_g': 1.0 + nrm(ks[15], (D_MODEL,), 0.02),
    }


def reference(x, mix_norm_g, w_in, sgu_norm_g, sgu_norm_b, w_spatial, b_spatial, lower_bounds,
              hgrn_norm_g, w_branch_a, w_branch_b, w_out, mlp_norm_g, w_mlp_up, w_mlp_down,
              final_norm_g):
    lb_all = jnp.cumsum(jax.nn.softmax(lower_bounds.astype(jnp.float32), axis=0), axis=0)
    lb_all = lb_all - lb_all[0:1]
    for l in range(DEPTH):
        h = rms_norm(x, mix_norm_g[l])
        z = h @ w_in[l]
        parts = []
        start = 0
        for w in IN_WIDTHS:
            parts.append(z[..., start:start + w])
            start += w
        u, v, q_raw, f_raw, i_raw, g_raw, gate_a, gate_b = parts
        a = spatial_gating(jax.nn.gelu(u), jax.nn.gelu(v), sgu_norm_g[l], sgu_norm_b[l],
                           w_spatial[l], b_spatial[l])
        b = hgrn2_mix(q_raw, f_raw, i_raw, g_raw, lb_all[l], hgrn_norm_g[l])
        merged = (jax.nn.sigmoid(gate_a) * (a @ w_branch_a[l])
                  + jax.nn.sigmoid(gate_b) * (b @ w_branch_b[l]))
        x = x + merged @ w_out[l]
        h = rms_norm(x, mlp_norm_g[l])
        x = x + jnp.square(jax.nn.relu(h @ w_mlp_up[l])) @ w_mlp_down[l]
    return rms_norm(x, final_norm_g)
```

```python
import contextlib
import numpy as np
import concourse.bass as bass
import concourse.mybir as mybir
from concourse.bass_utils import run_bass_kernel_spmd

F32 = mybir.dt.float32
BF16 = mybir.dt.bfloat16
AF = mybir.ActivationFunctionType
ALU = mybir.AluOpType

D = 1024
NCH = 8
DFF = 4096
EPS = 1e-6
NPANEL = 152
INTERLEAVE = True
OLD_CHAIN = True

ENGINES = ("pe", "act", "dve", "pool", "sp")
SAFE_SAME = {"pe", "dve"}


class Op:
    __slots__ = ("eng", "fn", "deps", "signal", "token", "is_dma", "waits")

    def __init__(self, eng, fn, is_dma=False):
        self.eng = eng
        self.fn = fn
        self.deps = []
        self.signal = False
        self.token = None
        self.is_dma = is_dma
        self.waits = []


class Prog:
    def __init__(self, nc):
        self.nc = nc
        self.ops = {e: [] for e in ENGINES}
        self.last_writer = {}
        self.readers = {}
        self.dma_groups = {}
        self.all_ops = []

    def _track(self, op, reads, writes):
        deps = set()
        for r in reads:
            w = self.last_writer.get(r)
            if w is not None:
                deps.add(w)
        for w_ in writes:
            w = self.last_writer.get(w_)
            if w is not None:
                deps.add(w)
            for rd in self.readers.get(w_, ()):
                deps.add(rd)
        deps.discard(op)
        op.deps = list(deps)
        for r in reads:
            self.readers.setdefault(r, []).append(op)
        for w_ in writes:
            self.last_writer[w_] = op
            self.readers[w_] = []

    def op(self, eng, fn, reads=(), writes=()):
        o = Op(eng, fn)
        self._track(o, reads, writes)
        self.ops[eng].append(o)
        self.all_ops.append(o)
        return o

    def dma(self, eng, out, in_, reads=(), writes=(), group=None):
        cnt = self.dma_groups.get(group, 0) + 1
        self.dma_groups[group] = cnt

        def fn(e, out=out, in_=in_):
            return e.dma_start(out=out, in_=in_)

        o = Op(eng, fn, is_dma=True)
        o.token = (("dma", group), 16 * cnt)
        o.signal = True
        self._track(o, reads, writes)
        self.ops[eng].append(o)
        self.all_ops.append(o)
        return o

    @staticmethod
    def _skip(d, o):
        return (not d.is_dma) and (not o.is_dma) and d.eng == o.eng and d.eng in SAFE_SAME

    def finalize(self, final_waits=()):
        nc = self.nc
        for o in self.all_ops:
            for d in o.deps:
                if d.is_dma or self._skip(d, o):
                    continue
                d.signal = True
        for o in final_waits:
            o.signal = True
        for e in ENGINES:
            c = 0
            for o in self.ops[e]:
                if o.is_dma:
                    continue
                if o.signal:
                    c += 1
                    o.token = (("eng", e), c)
            assert c < 65000, (e, c)
        for g, c in self.dma_groups.items():
            assert 16 * c < 65000, (g, c)
        for e in ENGINES:
            seen = {}
            for o in self.ops[e]:
                need = {}
                for d in o.deps:
                    if self._skip(d, o):
                        continue
                    k, v = d.token
                    if seen.get(k, 0) >= v:
                        continue
                    if need.get(k, 0) < v:
                        need[k] = v
                for k, v in need.items():
                    seen[k] = v
                o.waits = list(need.items())
        sem_keys = [("eng", e) for e in ENGINES
                    if any(o.signal and not o.is_dma for o in self.ops[e])]
        sem_keys += [("dma", g) for g in self.dma_groups]
        with contextlib.ExitStack() as st:
            sems = {}
            for i, k in enumerate(sem_keys):
                sems[k] = st.enter_context(nc.semaphore("s%d" % i))
            block = st.enter_context(nc.Block())
            fin = [o.token for o in final_waits]

            def run(e_name, eng):
                for o in self.ops[e_name]:
                    for k, v in o.waits:
                        eng.wait_ge(sems[k], v)
                    ins = o.fn(eng)
                    if o.signal:
                        ins.then_inc(sems[o.token[0]], 16 if o.is_dma else 1)
                if e_name == "sp":
                    for k, v in fin:
                        eng.wait_ge(sems[k], v)

            @block.tensor
            def _(t):
                run("pe", t)

            @block.scalar
            def _(s):
                run("act", s)

            @block.vector
            def _(v):
                run("dve", v)

            @block.gpsimd
            def _(g):
                run("pool", g)

            @block.sync
            def _(s):
                run("sp", s)


class _Stop(Exception):
    pass


def build_program(S, L, T, NSLOT=10, debug=None):
    assert S % T == 0 and T % 512 == 0
    NT = S // T
    NB = T // 128
    NH = T // 512
    NCK = T // 64
    nc = bass.Bass("TRN2", target_bir_lowering=False)
    x_d = nc.dram_tensor("x", [128, NCH, S], F32, kind="ExternalInput").ap()
    wp_d = nc.dram_tensor("wp", [L, NPANEL, 128, 8, 128], F32, kind="ExternalInput").ap()
    ws_d = nc.dram_tensor("ws", [128, L, 8, 128], F32, kind="ExternalInput").ap()
    bs_d = nc.dram_tensor("bs", [L, 1024], F32, kind="ExternalInput").ap()
    sg_d = nc.dram_tensor("sgu_g", [L, 1024], F32, kind="ExternalInput").ap()
    sb_d = nc.dram_tensor("sgu_b", [L, 1024], F32, kind="ExternalInput").ap()
    pv_d = nc.dram_tensor("pvec", [128, 3 * L * 8 + 8 + L], F32, kind="ExternalInput").ap()
    y_d = nc.dram_tensor("y", [128, NCH, S], F32, kind="ExternalOutput").ap()

    with contextlib.ExitStack() as st:
        def sb(name, shape, dt):
            return st.enter_context(nc.sbuf_tensor(name, shape, dt))

        X = sb("X", [128, NCH, T], F32)
        H = sb("H", [128, NCH, T], BF16)
        R = sb("R", [128, 24, T], BF16)
        Sst = sb("Sst", [128, L, 8, 128], F32)
        SRall = sb("SRall", [128, T // 64, 128], BF16)
        PT2 = sb("PT2", [128, T], BF16)
        KTT2 = sb("KTT2", [128, T], BF16)
        ring = sb("ring", [128, NSLOT, 8, 128], BF16)
        FT = [sb("FT%d" % i, [128, T], F32) for i in range(5)]
        BT = [sb("BT%d" % i, [128, T], BF16) for i in range(8)]
        WS = sb("WS", [128, L, 8, 128], BF16)
        GBC = sb("GBC", [128, 1024], BF16)
        BBC = sb("BBC", [128, 1024], BF16)
        BSR = sb("BSR", [1, 1024], F32)
        M01 = sb("M01", [128, T], F32)
        PV = sb("PV", [128, 3 * L * 8 + 8 + L], F32)
        LB = sb("LB", [128, L, 8], F32)
        LBm1 = sb("LBm1", [128, L, 8], F32)
        OML = sb("OML", [128, L, 8], F32)
        EXPL = sb("EXPL", [128, L, 8], F32)
        small = sb("small", [128, 64], F32)
        EB2 = sb("EB2", [128, 2, NCK], F32)
        identf = sb("identf", [128, 128], F32)
        ident = sb("ident", [128, 128], BF16)
        maskf = sb("maskf", [128, 128], F32)
        onesD = sb("onesD", [128, 128], BF16)
        onesV = sb("onesV", [128, 128], BF16)
        ones1 = sb("ones1", [1, 128], F32)
        epsT = sb("epsT", [128, 1], F32)
        psum = st.enter_context(nc.psum_tensor("psum", [128, 8, 512], F32))

        P = Prog(nc)
        panel_order = []
        JK = BT[5]
        A_ = lambda g: R[:, g, :]
        Bb_ = lambda h: R[:, 8 + h, :]
        VTflats = {8: R[:, 8:16, :].rearrange("p c t -> p (c t)"), 16: R[:, 16:24, :].rearrange("p c t -> p (c t)")}

        def VT(blk, lo=0, hi=1024, p0=0, p1=128, reg=16):
            return VTflats[reg][p0:p1, blk * 1024 + lo: blk * 1024 + hi]

        def vt_res(blk, reg=16):
            a = (blk * 1024) // T
            b = (blk * 1024 + 1023) // T
            return [("R", reg + i) for i in range(a, b + 1)]

        MG_ = lambda m: R[:, 16 + m, :]
        HID_ = lambda j: R[:, j, :]

        def bank(i):
            return psum[:, i, :]

        def pair(q):
            return psum[:, 2 * q:2 * q + NH, :].rearrange("p b f -> p (b f)")

        def pair_res(q):
            return [("b", 2 * q + i) for i in range(NH)]

        qstate = {"q": 0, "s": 0, "c": 0}

        def next_pair():
            q = qstate["q"]
            if qstate.get("hmode"):
                q = q % 2
                qstate["q"] = (q + 1) % 2
            else:
                qstate["q"] = (q + 1) % 3
            return q

        def next_single():
            s_ = qstate["s"]
            qstate["s"] = 1 - s_
            return 6 + s_

        def next_chain():
            c = qstate["c"]
            qstate["c"] = 1 - c
            return 6 + c

        uses = [(l, pi) for _t in range(NT) for l in range(L) for pi in range(NPANEL)]
        wstate = {"next_load": 0, "next_use": 0}

        def load_more():
            while wstate["next_load"] < len(uses) and wstate["next_load"] < wstate["next_use"] + NSLOT - 1:
                u = wstate["next_load"]
                l, pi = uses[u]
                s_ = u % NSLOT
                P.dma("pool", ring[:, s_, :, :], wp_d[l, pi], writes=[("ring", s_)], group=("ring", s_))
                wstate["next_load"] += 1

        def next_panel(l, pi):
            u = wstate["next_use"]
            assert uses[u] == (l, pi), (uses[u], l, pi)
            load_more()
            wstate["next_use"] += 1
            s_ = u % NSLOT
            return s_

        def proj(slot, act, act_res, q, kcs=range(8), first=True, last=True, kofs=0):
            for _ in proj_g(slot, act, act_res, q, kcs, first, last, kofs):
                pass

        def proj_g(slot, act, act_res, q, kcs=range(8), first=True, last=True, kofs=0):
            for hf in range(NH):
                def fn(e, hf=hf):
                    r = None
                    for i, kc in enumerate(kcs):
                        r = e.matmul(psum[:, 2 * q + hf, :], ring[:, slot, kc, :],
                                     act[:, kofs + kc, hf * 512:(hf + 1) * 512],
                                     start=(first and i == 0), stop=(last and i == len(kcs) - 1))
                    return r
                P.op("pe", fn, reads=[("ring", slot)] + act_res, writes=[("b", 2 * q + hf)])
                yield

        P.dma("sp", PV[:], pv_d, writes=["PV"], group="c0")
        P.op("pool", lambda e: e.memset(identf[:], 1.0), writes=["identf"])
        P.op("pool", lambda e: e.affine_select(out=identf[:], in_=identf[:], pattern=[[-1, 128]],
                                               compare_op=ALU.is_equal, fill=0.0, base=0, channel_multiplier=1),
             reads=["identf"], writes=["identf"])
        P.op("dve", lambda e: e.tensor_copy(out=ident[:], in_=identf[:]), reads=["identf"], writes=["ident"])
        P.op("pool", lambda e: e.memset(maskf[:], 1.0), writes=["maskf"])
        P.op("pool", lambda e: e.affine_select(out=maskf[:], in_=maskf[:], pattern=[[1, 128]],
                                               compare_op=ALU.is_ge, fill=0.0, base=0, channel_multiplier=-1),
             reads=["maskf"], writes=["maskf"])
        P.op("dve", lambda e: e.memset(onesD[:], 1.0 / 1024.0), writes=["onesD"])
        P.op("dve", lambda e: e.memset(onesV[:], 1.0 / 128.0), writes=["onesV"])
        P.op("dve", lambda e: e.memset(ones1[:], 1.0), writes=["ones1"])
        P.op("dve", lambda e: e.memset(epsT[:], EPS), writes=["epsT"])
        P.op("dve", lambda e: e.memset(M01[:], 1.0), writes=["M01"])
        P.op("dve", lambda e: e.memset(M01[:].rearrange("p (c k) -> p c k", k=64)[:, :, 0:1], 0.0),
             reads=["M01"], writes=["M01"])
        P.op("dve", lambda e: e.memset(Sst[:], 0.0), writes=[("S", l_, h_) for l_ in range(L) for h_ in range(8)])
        for l in range(L):
            XS = X[:, 0:2, :].rearrange("p c t -> p (c t)")[:, 0:1024]
            P.dma("sp", XS, ws_d[:, l].rearrange("p g t -> p (g t)"), writes=[("X", 0), ("X", 1)], group="c1")
            P.op("dve", lambda e, l=l, XS=XS: e.tensor_tensor(
                out=WS[:, l], in0=XS.rearrange("p (g t) -> p g t", g=8),
                in1=maskf[:].unsqueeze(1).broadcast_to([128, 8, 128]), op=ALU.mult),
                reads=[("X", 0), ("X", 1), "maskf"], writes=["WS"])
        P.op("dve", lambda e: e.memset(maskf[0:64, 64:128], 0.0), reads=["maskf", "WS"], writes=["maskf"])
        pv_lb = PV[:, 2 * L * 8:3 * L * 8].rearrange("p (l c) -> p l c", l=L)
        P.op("act", lambda e: e.activation(out=EXPL[:], in_=pv_lb, func=AF.Exp), reads=["PV"], writes=["EXPL"])

        def pl(fn, reads, writes):
            P.op("pool", fn, reads=reads, writes=writes)
        pl(lambda e: e.tensor_copy(out=small[:, 0:8], in_=EXPL[:, 0, :]), ["EXPL"], ["sm0"])
        for l in range(1, L):
            pl(lambda e, l=l: e.tensor_tensor(out=small[:, 0:8], in0=small[:, 0:8], in1=EXPL[:, l, :], op=ALU.add), ["EXPL", "sm0"], ["sm0"])
        P.op("dve", lambda e: e.reciprocal(out=small[:, 8:16], in_=small[:, 0:8]), reads=["sm0"], writes=["sm8"])
        pl(lambda e: e.memset(LB[:, 0, :], 0.0), [], [("LBl", 0)])
        for l in range(1, L):
            pl(lambda e, l=l: e.tensor_tensor(out=small[:, 16:24], in0=EXPL[:, l, :], in1=small[:, 8:16], op=ALU.mult), ["EXPL", "sm8"], ["sm16"])
            pl(lambda e, l=l: e.tensor_tensor(out=LB[:, l, :], in0=LB[:, l - 1, :], in1=small[:, 16:24], op=ALU.add), ["sm16", ("LBl", l - 1)], [("LBl", l)])
        pl(lambda e: e.tensor_scalar(out=LBm1[:], in0=LB[:], scalar1=-1.0, scalar2=None, op0=ALU.add), [("LBl", l) for l in range(L)], ["LBm1"])
        pl(lambda e: e.tensor_scalar(out=OML[:], in0=LBm1[:], scalar1=-1.0, scalar2=None, op0=ALU.mult), ["LBm1"], ["LB"])

        mixg = lambda l: PV[:, l * 8:(l + 1) * 8]
        mlpg = lambda l: PV[:, L * 8 + l * 8: L * 8 + (l + 1) * 8]
        fing = PV[:, 3 * L * 8:3 * L * 8 + 8]
        hng = lambda l: PV[:, 3 * L * 8 + 8 + l: 3 * L * 8 + 8 + l + 1]

        XR = [("X", c) for c in range(8)]
        HR = [("H", c) for c in range(8)]

        def rmsnorm(gains, to_x=False):
            q = next_pair()
            for c in range(8):
                bt = c % 2
                if bt == 0:
                    P.op("act", lambda e, c=c, bt=bt: e.activation(out=BT[bt][:], in_=X[:, c, :], func=AF.Square),
                         reads=[("X", c)], writes=[("BT", bt)])
                else:
                    P.op("pool", lambda e, c=c, bt=bt: e.tensor_tensor(out=BT[bt][:], in0=X[:, c, :], in1=X[:, c, :], op=ALU.mult),
                         reads=[("X", c)], writes=[("BT", bt)])
                for hf in range(NH):
                    P.op("pe", lambda e, c=c, bt=bt, hf=hf: e.matmul(
                        psum[:, 2 * q + hf, :], onesD[:], BT[bt][:, hf * 512:(hf + 1) * 512],
                        start=(c == 0), stop=(c == 7)),
                        reads=[("BT", bt), "onesD"], writes=[("b", 2 * q + hf)])
            RS = FT[4]
            P.op("act", lambda e: e.activation(out=RS[:], in_=pair(q), func=AF.Ln, bias=epsT[:]),
                 reads=["epsT"], writes=pair_res(q) + [("FT", 4)])
            P.op("act", lambda e: e.activation(out=RS[:], in_=RS[:], func=AF.Exp, scale=-0.5),
                 reads=[], writes=[("FT", 4)])
            for c in range(8):
                eng_ = "dve"
                if to_x:
                    P.op(eng_, lambda e, c=c: e.scalar_tensor_tensor(
                        out=X[:, c, :], in0=X[:, c, :], scalar=gains[:, c:c + 1], in1=RS[:],
                        op0=ALU.mult, op1=ALU.mult), reads=[("FT", 4), "PV"], writes=[("X", c)])
                else:
                    P.op(eng_, lambda e, c=c: e.scalar_tensor_tensor(
                        out=H[:, c, :], in0=X[:, c, :], scalar=gains[:, c:c + 1], in1=RS[:],
                        op0=ALU.mult, op1=ALU.mult), reads=[("FT", 4), "PV", ("X", c)], writes=[("H", c)])

        def to_token_major(src_bt, c, reg=16):
            for b0 in range(0, NB, 8):
                nb = min(8, NB - b0)
                sbk = next_single()
                pT = bank(sbk).bitcast(BF16)

                def fn(e, b0=b0, nb=nb, pT=pT):
                    r = None
                    for j in range(nb):
                        r = e.transpose(out=pT[:, j * 128:(j + 1) * 128],
                                        in_=BT[src_bt][:, (b0 + j) * 128:(b0 + j + 1) * 128], identity=ident[:])
                    return r
                P.op("pe", fn, reads=[("BT", src_bt), "ident"], writes=[("b", sbk)])
                res = []
                for j in range(nb):
                    res += vt_res(b0 + j, reg)

                def ev(e, b0=b0, nb=nb, pT=pT):
                    dst = VTflats[reg][:, b0 * 1024:(b0 + nb) * 1024].rearrange("p (j f) -> p j f", f=1024)[:, :, c * 128:(c + 1) * 128]
                    return e.tensor_copy(out=dst, in_=pT[:, 0:nb * 128].rearrange("p (j f) -> p j f", f=128))
                P.op("dve", ev, reads=[], writes=[("b", sbk)] + sorted(set(res)))

        def dump(src_fn):
            for c in range(8):
                P.op("dve", lambda e, c=c: e.tensor_copy(out=X[:, c, :], in_=src_fn(c)),
                     reads=HR + [("R", i) for i in range(24)], writes=[("X", c)])
            raise _Stop()

        def layer(l):
            base = {"pi": 0}

            def panel(key):
                pi = base["pi"]
                base["pi"] += 1
                if len(panel_order) < NPANEL:
                    panel_order.append(key)
                else:
                    assert panel_order[pi] == key, (pi, key, panel_order[pi])
                return next_panel(l, pi)

            P.dma("pool", GBC[:], sg_d[l].partition_broadcast(128), writes=["GBC"], group="gbc")
            P.dma("pool", BBC[:], sb_d[l].partition_broadcast(128), writes=["BBC"], group="gbc")
            P.dma("sp", BSR[:], bs_d[l:l + 1, :], writes=["BSR"], group="bsr")

            rmsnorm(mixg(l))

            if debug == "norm":
                dump(lambda c: H[:, c, :])
            for c in range(8):
                s_ = panel(('v', c))
                q = next_pair()
                proj(s_, H, HR, q)
                bt = c % 2
                P.op("act", lambda e, q=q, bt=bt: e.activation(out=BT[bt][:], in_=pair(q), func=AF.Gelu_apprx_tanh),
                     reads=[], writes=pair_res(q) + [("BT", bt)])
                to_token_major(bt, c, reg=8)
            def zacc(e):
                e.memset(small[:, 24:24 + NB], 0.0)
                return e.memset(small[:, 40:40 + NB], 0.0)
            P.op("dve", zacc, reads=[], writes=[("sm", 24 + b) for b in range(NB)] + [("sm", 40 + b) for b in range(NB)] + ["lnstat", "lnstat2", "lnstat3", "ln_mean", "ln_m2", "ln_e2"])
            for c in range(8):
                s_ = panel(('i', c))
                q = next_pair()
                proj(s_, H, HR, q)
                bt = c % 2
                P.op("act", lambda e, q=q, bt=bt: e.activation(out=BT[bt][:], in_=pair(q), func=AF.Copy),
                     reads=[], writes=pair_res(q) + [("BT", bt)])
                to_token_major(bt, c, reg=16)
                for blk in range(c * NB // 8, (c + 1) * NB // 8):
                    P.op("act", lambda e, blk=blk: e.activation(out=JK[:], in_=VT(blk, reg=8), func=AF.Copy,
                                                              accum_out=small[:, 24 + blk:25 + blk]),
                         reads=vt_res(blk, 8), writes=[("BT", 5), ("sm", 24 + blk)])
                    P.op("act", lambda e, blk=blk: e.activation(out=JK[:], in_=VT(blk, reg=8), func=AF.Square,
                                                              accum_out=small[:, 40 + blk:41 + blk]),
                         reads=vt_res(blk, 8), writes=[("BT", 5), ("sm", 40 + blk)])

            smr = [("sm", 24 + b) for b in range(NB)] + [("sm", 40 + b) for b in range(NB)]
            P.op("pool", lambda e: e.tensor_scalar(out=small[:, 24:24 + NB], in0=small[:, 24:24 + NB], scalar1=1.0 / 1024, scalar2=None, op0=ALU.mult),
                 reads=smr, writes=["ln_mean"])
            P.op("pool", lambda e: e.tensor_tensor(out=small[:, 56:56 + NB], in0=small[:, 24:24 + NB], in1=small[:, 24:24 + NB], op=ALU.mult),
                 reads=["ln_mean"], writes=["ln_m2"])
            P.op("pool", lambda e: e.tensor_scalar(out=small[:, 40:40 + NB], in0=small[:, 40:40 + NB], scalar1=1.0 / 1024, scalar2=None, op0=ALU.mult),
                 reads=smr, writes=["ln_e2"])
            P.op("pool", lambda e: e.tensor_tensor(out=small[:, 40:40 + NB], in0=small[:, 40:40 + NB], in1=small[:, 56:56 + NB], op=ALU.subtract),
                 reads=["ln_e2", "ln_m2"], writes=["ln_e2"])
            P.op("pool", lambda e: e.tensor_scalar(out=small[:, 40:40 + NB], in0=small[:, 40:40 + NB], scalar1=EPS, scalar2=None, op0=ALU.add),
                 reads=["ln_e2"], writes=["ln_e2", "lnstat"])
            P.op("act", lambda e: e.activation(out=small[:, 40:40 + NB], in_=small[:, 40:40 + NB], func=AF.Ln),
                 reads=["lnstat"], writes=["lnstat2"])
            P.op("act", lambda e: e.activation(out=small[:, 40:40 + NB], in_=small[:, 40:40 + NB], func=AF.Exp, scale=-0.5),
                 reads=["lnstat2"], writes=["lnstat3"])
            for blk in range(NB):
                def lnap(e, blk=blk):
                    e.tensor_scalar(out=VT(blk, reg=8), in0=VT(blk, reg=8), scalar1=small[:, 24 + blk:25 + blk],
                                    scalar2=small[:, 40 + blk:41 + blk], op0=ALU.subtract, op1=ALU.mult)
                    e.tensor_tensor(out=VT(blk, reg=8), in0=VT(blk, reg=8), in1=GBC[:], op=ALU.mult)
                    return e.tensor_tensor(out=VT(blk, reg=8), in0=VT(blk, reg=8), in1=BBC[:], op=ALU.add)
                P.op("dve", lnap, reads=["lnstat3", "ln_mean", "GBC", "BBC"], writes=vt_res(blk, 8))

            def phaseU():
                for g in range(8):
                    s_ = panel(('u', g))
                    q = next_pair()
                    for _ in proj_g(s_, H, HR, q):
                        yield
                    bt = g % 2
                    P.op("act", lambda e, q=q, bt=bt: e.activation(out=BT[bt][:], in_=pair(q), func=AF.Gelu_apprx_tanh),
                         reads=[], writes=pair_res(q) + [("BT", bt)])
                    yield
                    qm = next_pair()
                    for hf in range(NH):
                        def mix(e, hf=hf, g=g, qm=qm):
                            r = None
                            for j in range(4):
                                blk = hf * 4 + j
                                e.matmul(psum[:, 2 * qm + hf, j * 128:(j + 1) * 128], ones1[0:1, :],
                                         BSR[0:1, g * 128:(g + 1) * 128], start=(j == 0), stop=False)
                                r = e.matmul(psum[:, 2 * qm + hf, j * 128:(j + 1) * 128],
                                             VT(blk, g * 128, (g + 1) * 128, reg=8), WS[:, l, g, :],
                                             start=False, stop=(j == 3))
                            return r
                        res = []
                        for j in range(4):
                            res += vt_res(hf * 4 + j, 8)
                        P.op("pe", mix, reads=sorted(set(res)) + ["WS", "BSR", "ones1"], writes=[("b", 2 * qm + hf)])
                        yield
                    P.op("dve", lambda e, g=g, qm=qm, bt=bt: e.tensor_tensor(out=A_(g), in0=pair(qm), in1=BT[bt][:], op=ALU.mult),
                         reads=[("BT", bt)], writes=pair_res(qm) + [("R", g)])
                    yield

            SG, LF, BC, DD, RS = FT[0], FT[1], FT[2], FT[3], FT[4]
            KT, OSQ = BT[2], BT[5]
            QTs, GSs, PTss, KTTs = [BT[3], BT[0]], [BT[4], BT[1]], [BT[6], PT2], [BT[7], KTT2]
            QTr, GSr, PTr, KTr = [("BT", 3), ("BT", 0)], [("BT", 4), ("BT", 1)], [("BT", 6), "PT2"], [("BT", 7), "KTT2"]
            EBr = [("EB", 0), ("EB", 1)]

            def prepA(hh):
                p = hh % 2
                QT, GS, PTs, KTT = QTs[p], GSs[p], PTss[p], KTTs[p]
                EBp = EB2[:, p, :]
                sf_ = panel(('f', hh))
                qf = next_pair()
                for _ in proj_g(sf_, H, HR, qf):
                    yield
                P.op("act", lambda e, qf=qf: e.activation(out=SG[:], in_=pair(qf), func=AF.Sigmoid),
                     reads=[], writes=pair_res(qf) + [("FT", 0)])
                yield
                P.op("act", lambda e, hh=hh: e.activation(out=LF[:], in_=SG[:], func=AF.Ln,
                                                          scale=OML[:, l, hh:hh + 1], bias=LB[:, l, hh:hh + 1]),
                     reads=[("FT", 0), "LB"], writes=[("FT", 1)])
                yield
                P.op("dve", lambda e: e.tensor_tensor_scan(out=BC[:], data0=M01[:], data1=LF[:], initial=0.0,
                                                           op0=ALU.mult, op1=ALU.add),
                     reads=[("FT", 1), "M01"], writes=[("FT", 2)])
                yield
                BCv = BC[:].rearrange("p (c k) -> p c k", k=64)
                P.op("dve", lambda e, BCv=BCv: e.tensor_tensor(
                    out=DD[:].rearrange("p (c k) -> p c k", k=64), in0=BCv[:, :, 63:64].broadcast_to([128, NCK, 64]),
                    in1=BCv, op=ALU.subtract), reads=[("FT", 2)], writes=[("FT", 3)])
                yield
                P.op("act", lambda e, BCv=BCv, EBp=EBp: e.activation(out=EBp.unsqueeze(2), in_=BCv[:, :, 63:64], func=AF.Exp),
                     reads=[("FT", 2)], writes=[EBr[p]])
                P.op("dve", lambda e, hh=hh: e.tensor_scalar(out=SG[:], in0=SG[:], scalar1=-1.0, scalar2=LBm1[:, l, hh:hh + 1],
                                                             op0=ALU.add, op1=ALU.mult),
                     reads=[("FT", 1), "LB"], writes=[("FT", 0)])
                yield
                P.op("act", lambda e: e.activation(out=LF[:], in_=DD[:], func=AF.Exp),
                     reads=[("FT", 3), ("FT", 2)], writes=[("FT", 1)])
                yield
                P.op("dve", lambda e: e.tensor_tensor(out=KT[:], in0=SG[:], in1=LF[:], op=ALU.mult),
                     reads=[("FT", 0), ("FT", 1)], writes=[("BT", 2)])
                yield
                P.op("act", lambda e: e.activation(out=DD[:], in_=DD[:], func=AF.Exp, scale=-1.0),
                     reads=[], writes=[("FT", 3)])
                yield
                sq_ = panel(('q', hh))
                qq = next_pair()
                for _ in proj_g(sq_, H, HR, qq):
                    yield
                P.op("act", lambda e, qq=qq: e.activation(out=BC[:], in_=pair(qq), func=AF.Silu),
                     reads=[EBr[p]], writes=pair_res(qq) + [("FT", 2)])
                yield
                P.op("dve", lambda e, QT=QT: e.tensor_tensor(out=QT[:], in0=BC[:], in1=DD[:], op=ALU.mult),
                     reads=[("FT", 2), ("FT", 3)], writes=[QTr[p]])
                yield
                sg_ = panel(('g', hh))
                qg = next_pair()
                for _ in proj_g(sg_, H, HR, qg):
                    yield
                P.op("act", lambda e, qg=qg, GS=GS: e.activation(out=GS[:], in_=pair(qg), func=AF.Silu),
                     reads=[], writes=pair_res(qg) + [GSr[p]])
                yield
                sbk = next_single()
                pT = bank(sbk).bitcast(BF16)

                def ktr(e, pT=pT):
                    r = None
                    for j in range(NB):
                        r = e.transpose(out=pT[:, j * 128:(j + 1) * 128],
                                        in_=KT[:, j * 128:(j + 1) * 128], identity=ident[:])
                    return r
                P.op("pe", ktr, reads=[("BT", 2), "ident"], writes=[("b", sbk)])
                P.op("dve", lambda e, pT=pT, KTT=KTT: e.tensor_copy(out=KTT[:, 0:NB * 128], in_=pT[:, 0:NB * 128]),
                     reads=[], writes=[("b", sbk), KTr[p]])
                yield
                qs = next_pair()
                for hf in range(NH):
                    def sc(e, hf=hf, qs=qs, QT=QT):
                        r = None
                        for j in range(4):
                            blk = hf * 4 + j
                            r = e.matmul(psum[:, 2 * qs + hf, j * 128:(j + 1) * 128],
                                         KT[:, blk * 128:(blk + 1) * 128], QT[:, blk * 128:(blk + 1) * 128],
                                         start=(j == 0), stop=(j == 3))
                        return r
                    P.op("pe", sc, reads=[("BT", 2), QTr[p]], writes=[("b", 2 * qs + hf)])
                    yield
                P.op("dve", lambda e, qs=qs, PTs=PTs: e.tensor_tensor(
                    out=PTs[:].rearrange("p (j t) -> p j t", t=128), in0=pair(qs).rearrange("p (j t) -> p j t", t=128),
                    in1=maskf[:].unsqueeze(1).broadcast_to([128, NB, 128]), op=ALU.mult),
                    reads=["maskf"], writes=pair_res(qs) + [PTr[p]])
                yield

            def chainB(hh):
                if OLD_CHAIN:
                    yield from chainB_old(hh)
                else:
                    yield from chainB_new(hh)

            def chainB_old(hh):
                p = hh % 2
                QT, GS, PTs, KTT = QTs[p], GSs[p], PTss[p], KTTs[p]
                Sv = Sst[:, l, hh, :]
                qo = 2
                for blk in range(NB):
                    hf, j = blk // 4, blk % 4
                    vals = lambda p0, p1, blk=blk, hh=hh: VT(blk, hh * 128, (hh + 1) * 128, p0, p1)
                    P.op("pe", lambda e, blk=blk, hf=hf, j=j, vals=vals, PTs=PTs: e.matmul(
                        psum[:, 2 * qo + hf, j * 128:(j + 1) * 128], vals(0, 128), PTs[:, blk * 128:(blk + 1) * 128],
                        start=(j == 0), stop=False),
                        reads=vt_res(blk) + [PTr[p]], writes=[("b", 2 * qo + hf)])
                    for ck in range(2):
                        c = blk * 2 + ck
                        par = c % 2
                        p0, p1 = ck * 64, ck * 64 + 64
                        P.op("act", lambda e, c=c, par=par, Sv=Sv, p=p: e.activation(out=SRall[:, par, :], in_=Sv, func=AF.Copy, scale=EB2[:, p, c:c + 1]),
                             reads=[EBr[p], ("S", l, hh)], writes=[("SR", par)])
                        P.op("pe", lambda e, blk=blk, hf=hf, j=j, ck=ck, par=par, QT=QT: e.matmul(
                            psum[:, 2 * qo + hf, j * 128 + ck * 64: j * 128 + ck * 64 + 64], SRall[:, par, :],
                            QT[:, blk * 128 + ck * 64: blk * 128 + ck * 64 + 64],
                            start=False, stop=(ck == 1 and j == 3)),
                            reads=[("SR", par), QTr[p]], writes=[("b", 2 * qo + hf)])
                        cb = next_chain()
                        P.op("pe", lambda e, blk=blk, p0=p0, p1=p1, cb=cb, vals=vals, KTT=KTT: e.matmul(
                            psum[:, cb, 0:128], KTT[p0:p1, blk * 128:(blk + 1) * 128], vals(p0, p1),
                            start=True, stop=True),
                            reads=vt_res(blk) + [KTr[p]], writes=[("b", cb)])
                        P.op("dve", lambda e, cb=cb, c=c, Sv=Sv, p=p: e.scalar_tensor_tensor(out=Sv, in0=Sv, scalar=EB2[:, p, c:c + 1], in1=psum[:, cb, 0:128],
                                                                                      op0=ALU.mult, op1=ALU.add),
                             reads=[EBr[p]], writes=[("b", cb), ("S", l, hh)])
                        yield
                yield from chain_end(hh)

            def chainB_new(hh):
                p = hh % 2
                QT, GS, PTs, KTT = QTs[p], GSs[p], PTss[p], KTTs[p]
                Sv = Sst[:, l, hh, :]
                qo = 2
                vals = lambda blk, p0, p1, hh=hh: VT(blk, hh * 128, (hh + 1) * 128, p0, p1)
                P.op("dve", lambda e, Sv=Sv, p=p: e.tensor_scalar(out=SRall[:, 0, :], in0=Sv, scalar1=EB2[:, p, 0:1], scalar2=None, op0=ALU.mult),
                     reads=[EBr[p], ("S", l, hh)], writes=[("SRs", 0)])
                yield
                for hf in range(NH):
                    def intra(e, hf=hf, PTs=PTs, vals=vals):
                        r = None
                        for j in range(4):
                            blk = hf * 4 + j
                            r = e.matmul(psum[:, 2 * qo + hf, j * 128:(j + 1) * 128], vals(blk, 0, 128),
                                         PTs[:, blk * 128:(blk + 1) * 128], start=(j == 0), stop=False)
                        return r
                    res = []
                    for j in range(4):
                        res += vt_res(hf * 4 + j)
                    P.op("pe", intra, reads=sorted(set(res)) + [PTr[p]], writes=[("b", 2 * qo + hf)])
                    yield
                for g in range(NCK // 4):
                    cb = 6 + g % 2

                    def ugrp(e, g=g, cb=cb, KTT=KTT, vals=vals):
                        r = None
                        for i in range(4):
                            c = 4 * g + i
                            blk, ck = c // 2, c % 2
                            p0 = ck * 64
                            r = e.matmul(psum[:, cb, i * 128:(i + 1) * 128], KTT[p0:p0 + 64, blk * 128:(blk + 1) * 128],
                                         vals(blk, p0, p0 + 64), start=True, stop=True)
                        return r
                    P.op("pe", ugrp, reads=vt_res(2 * g) + vt_res(2 * g + 1) + [KTr[p]], writes=[("b", cb)])
                    yield

                    def scan(e, g=g, cb=cb, Sv=Sv, p=p):
                        r = None
                        for i in range(4):
                            c = 4 * g + i
                            for hv in range(2):
                                r = e.scalar_tensor_tensor(out=Sv[:, hv * 64:hv * 64 + 64], in0=Sv[:, hv * 64:hv * 64 + 64],
                                                           scalar=EB2[:, p, c:c + 1],
                                                           in1=psum[:, cb, i * 128 + hv * 64:i * 128 + hv * 64 + 64],
                                                           op0=ALU.mult, op1=ALU.add)
                            if c + 1 < NCK:
                                for hv in range(2):
                                    r = e.tensor_scalar(out=SRall[:, c + 1, hv * 64:hv * 64 + 64], in0=Sv[:, hv * 64:hv * 64 + 64],
                                                        scalar1=EB2[:, p, c + 1:c + 2], scalar2=None, op0=ALU.mult)
                        return r
                    P.op("dve", scan, reads=[EBr[p]], writes=[("b", cb), ("S", l, hh), ("SRs", g + 1)])
                    yield

                    def inter(e, g=g, QT=QT):
                        r = None
                        for i in range(4):
                            c = 4 * g + i
                            blk, ck = c // 2, c % 2
                            hf, j = blk // 4, blk % 4
                            r = e.matmul(psum[:, 2 * qo + hf, j * 128 + ck * 64:j * 128 + ck * 64 + 64], SRall[:, c, :],
                                         QT[:, blk * 128 + ck * 64:blk * 128 + ck * 64 + 64], start=False, stop=(i == 3))
                        return r
                    P.op("pe", inter, reads=[("SRs", g), ("SRs", g + 1), QTr[p]], writes=[("b", 2 * qo + (2 * g) // 4)])
                    yield
                yield from chain_end(hh)

            def chain_end(hh):
                p = hh % 2
                GS = GSs[p]
                qo = 2
                P.op("act", lambda e: e.activation(out=OSQ[:], in_=pair(qo), func=AF.Square),
                     reads=[], writes=pair_res(qo) + [("BT", 5)])
                yield
                qn = next_pair()
                for hf in range(NH):
                    P.op("pe", lambda e, hf=hf, qn=qn: e.matmul(psum[:, 2 * qn + hf, :], onesV[:],
                                                               OSQ[:, hf * 512:(hf + 1) * 512], start=True, stop=True),
                         reads=[("BT", 5), "onesV"], writes=[("b", 2 * qn + hf)])
                yield
                P.op("act", lambda e, qn=qn: e.activation(out=RS[:], in_=pair(qn), func=AF.Ln, bias=epsT[:]),
                     reads=["epsT"], writes=pair_res(qn) + [("FT", 4)])
                P.op("act", lambda e: e.activation(out=RS[:], in_=RS[:], func=AF.Exp, scale=-0.5),
                     reads=[], writes=[("FT", 4)])
                yield
                P.op("dve", lambda e: e.scalar_tensor_tensor(
                    out=RS[:], in0=pair(qo), scalar=hng(l), in1=RS[:], op0=ALU.mult, op1=ALU.mult),
                    reads=["PV"], writes=pair_res(qo) + [("FT", 4)])
                P.op("dve", lambda e, hh=hh, GS=GS: e.tensor_tensor(out=Bb_(hh), in0=RS[:], in1=GS[:], op=ALU.mult),
                     reads=[("FT", 4), GSr[p]], writes=[("R", 8 + hh)])
                yield

            gU = phaseU()
            gA0 = prepA(0)
            for _ in gU:
                next(gA0, None)
            for _ in gA0:
                pass
            qstate["hmode"] = True
            for hh in range(8):
                gB = chainB(hh)
                gA = prepA(hh + 1) if hh < 7 else iter(())
                if INTERLEAVE:
                    for _ in gB:
                        next(gA, None)
                    for _ in gA:
                        pass
                else:
                    for _ in gB:
                        pass
                    for _ in gA:
                        pass
            qstate["hmode"] = False

            if debug == "b":
                dump(lambda c: R[:, 8 + c, :])
            AR = [("R", g) for g in range(8)]
            BR = [("R", 8 + g) for g in range(8)]
            for m in range(8):
                sga = panel(('ga', m))
                q1 = next_pair()
                proj(sga, H, HR, q1)
                P.op("act", lambda e, q1=q1: e.activation(out=FT[0][:], in_=pair(q1), func=AF.Sigmoid),
                     reads=[], writes=pair_res(q1) + [("FT", 0)])
                sgb = panel(('gb', m))
                q2 = next_pair()
                proj(sgb, H, HR, q2)
                P.op("act", lambda e, q2=q2: e.activation(out=FT[1][:], in_=pair(q2), func=AF.Sigmoid),
                     reads=[], writes=pair_res(q2) + [("FT", 1)])
                sa = panel(('pa', m))
                q3 = next_pair()
                proj(sa, R, AR, q3)
                P.op("dve", lambda e, q3=q3: e.tensor_tensor(out=FT[0][:], in0=pair(q3), in1=FT[0][:], op=ALU.mult),
                     reads=[], writes=pair_res(q3) + [("FT", 0)])
                sbp = panel(('pb', m))
                q4 = next_pair()
                proj(sbp, R, BR, q4, kofs=8)
                P.op("dve", lambda e, q4=q4: e.tensor_tensor(out=FT[1][:], in0=pair(q4), in1=FT[1][:], op=ALU.mult),
                     reads=[], writes=pair_res(q4) + [("FT", 1)])
                P.op("dve", lambda e, m=m: e.tensor_tensor(out=MG_(m), in0=FT[0][:], in1=FT[1][:], op=ALU.add),
                     reads=[("FT", 0), ("FT", 1)], writes=[("R", 16 + m)])
            if debug == "merged":
                dump(lambda c: R[:, 16 + c, :])
            MR = [("R", 16 + g) for g in range(8)]
            for m in range(8):
                s_ = panel(('o', m))
                q = next_pair()
                proj(s_, R, MR, q, kofs=16)
                P.op("dve", lambda e, q=q, m=m: e.tensor_tensor(out=X[:, m, :], in0=pair(q), in1=X[:, m, :], op=ALU.add),
                     reads=[], writes=pair_res(q) + [("X", m)])

            if debug == "mixer":
                raise _Stop()
            rmsnorm(mlpg(l))
            for half in range(2):
                for j in range(16):
                    s_ = panel(('up', half * 16 + j))
                    q = next_pair()
                    proj(s_, H, HR, q)
                    P.op("act", lambda e, q=q, j=j: e.activation(out=HID_(j), in_=pair(q), func=AF.Relu),
                         reads=[], writes=pair_res(q) + [("R", j)])
                    P.op("dve", lambda e, j=j: e.tensor_tensor(out=HID_(j), in0=HID_(j), in1=HID_(j), op=ALU.mult),
                         reads=[], writes=[("R", j)])
                HIDR = [("R", j) for j in range(16)]
                for m in range(8):
                    q = next_pair()
                    s0 = panel(('dn', half, 0, m))
                    proj(s0, R, HIDR, q, first=True, last=False, kofs=0)
                    s1 = panel(('dn', half, 1, m))
                    proj(s1, R, HIDR, q, first=False, last=True, kofs=8)
                    P.op("dve", lambda e, q=q, m=m: e.tensor_tensor(out=X[:, m, :], in0=pair(q), in1=X[:, m, :], op=ALU.add),
                         reads=[], writes=pair_res(q) + [("X", m)])
            assert base["pi"] == NPANEL

        last = None
        for t in range(NT):
            for c in range(8):
                P.dma("sp", X[:, c, :], x_d[:, c, t * T:(t + 1) * T], writes=[("X", c)], group=("xin", c))
            try:
                for l in range(L):
                    layer(l)
                rmsnorm(fing, to_x=True)
            except _Stop:
                pass
            lasts = []
            for c in range(8):
                lasts.append(P.dma("sp", y_d[:, c, t * T:(t + 1) * T], X[:, c, :], reads=[("X", c)], group=("yout", c)))
        P.finalize(final_waits=lasts)
    nc._panel_order = list(panel_order)
    return nc


def _panelize(w):
    k, n = w.shape
    assert k == 1024
    return np.ascontiguousarray(w.reshape(8, 128, n // 128, 128).transpose(2, 1, 0, 3))


def prep_weights(order, w_in, w_branch_a, w_branch_b, w_out, w_mlp_up, w_mlp_down):
    L = w_in.shape[0]
    out = np.empty((L, NPANEL, 128, 8, 128), np.float32)
    base = {'u': 0, 'v': 8, 'q': 16, 'f': 24, 'i': 32, 'g': 40, 'ga': 48, 'gb': 56}
    assert len(order) == NPANEL and len(set(order)) == NPANEL
    for l in range(L):
        pin = _panelize(w_in[l])
        pa = _panelize(w_branch_a[l])
        pb = _panelize(w_branch_b[l])
        po = _panelize(w_out[l])
        pu = _panelize(w_mlp_up[l])
        wd = w_mlp_down[l]
        pd = [_panelize(wd[i * 1024:(i + 1) * 1024]) for i in range(4)]
        for i, key in enumerate(order):
            k0 = key[0]
            if k0 in base:
                out[l, i] = pin[base[k0] + key[1]]
            elif k0 == 'pa':
                out[l, i] = pa[key[1]]
            elif k0 == 'pb':
                out[l, i] = pb[key[1]]
            elif k0 == 'o':
                out[l, i] = po[key[1]]
            elif k0 == 'up':
                out[l, i] = pu[key[1]]
            elif k0 == 'dn':
                out[l, i] = pd[key[1] * 2 + key[2]][key[3]]
            else:
                raise KeyError(key)
    return out


def _pp(v):
    lead = v.shape[:-1]
    a = v.reshape(*lead, 8, 128)
    return np.moveaxis(a, -1, 0)


_CACHE = {}


def kernel(x, mix_norm_g, w_in, sgu_norm_g, sgu_norm_b, w_spatial, b_spatial, lower_bounds,
           hgrn_norm_g, w_branch_a, w_branch_b, w_out, mlp_norm_g, w_mlp_up, w_mlp_down,
           final_norm_g, T=1024, debug=None):
    x = np.asarray(x, np.float32)
    B, S, _ = x.shape
    L = int(np.asarray(w_in).shape[0])
    f = lambda a: np.asarray(a, np.float32)
    ws = np.ascontiguousarray(f(w_spatial).transpose(3, 0, 1, 2))
    bs = np.ascontiguousarray(f(b_spatial).reshape(L, 1024))
    pvec = np.concatenate([
        _pp(f(mix_norm_g)).reshape(128, L * 8),
        _pp(f(mlp_norm_g)).reshape(128, L * 8),
        _pp(f(lower_bounds)).reshape(128, L * 8),
        _pp(f(final_norm_g)).reshape(128, 8),
        np.ascontiguousarray(f(hgrn_norm_g).T),
    ], axis=1).astype(np.float32)
    pvec = np.ascontiguousarray(pvec)
    key = (S, L, T, debug)
    if key not in _CACHE:
        _CACHE[key] = build_program(S, L, T, debug=debug)
    nc = _CACHE[key]
    wp = prep_weights(nc._panel_order, f(w_in), f(w_branch_a), f(w_branch_b), f(w_out), f(w_mlp_up), f(w_mlp_down))
    in_maps = []
    for b in range(B):
        xb = np.ascontiguousarray(x[b].T.reshape(8, 128, S).transpose(1, 0, 2))
        in_maps.append({"x": xb, "wp": wp, "ws": ws, "bs": bs, "sgu_g": f(sgu_norm_g), "sgu_b": f(sgu_norm_b),
                        "pvec": pvec})
    res = run_bass_kernel_spmd(nc, in_maps, core_ids=list(range(B)))
    out = np.empty((B, S, D), np.float32)
    for b in range(B):
        yb = res.results[b]["y"]
        out[b] = yb.transpose(2, 1, 0).reshape(S, D)
    return out
```

```python
import contextlib
import numpy as np
import concourse.bass as bass
import concourse.mybir as mybir
from concourse.bass_utils import run_bass_kernel_spmd

F32 = mybir.dt.float32
BF16 = mybir.dt.bfloat16
AF = mybir.ActivationFunctionType
ALU = mybir.AluOpType

D = 1024
NCH = 8
DFF = 4096
EPS = 1e-6
NPANEL = 152
INTERLEAVE = True
OLD_CHAIN = True

ENGINES = ("pe", "act", "dve", "pool", "sp")
SAFE_SAME = {"pe", "dve"}


class Op:
    __slots__ = ("eng", "fn", "deps", "signal", "token", "is_dma", "waits")

    def __init__(self, eng, fn, is_dma=False):
        self.eng = eng
        self.fn = fn
        self.deps = []
        self.signal = False
        self.token = None
        self.is_dma = is_dma
        self.waits = []


class Prog:
    def __init__(self, nc):
        self.nc = nc
        self.ops = {e: [] for e in ENGINES}
        self.last_writer = {}
        self.readers = {}
        self.dma_groups = {}
        self.all_ops = []

    def _track(self, op, reads, writes):
        deps = set()
        for r in reads:
            w = self.last_writer.get(r)
            if w is not None:
                deps.add(w)
        for w_ in writes:
            w = self.last_writer.get(w_)
            if w is not None:
                deps.add(w)
            for rd in self.readers.get(w_, ()):
                deps.add(rd)
        deps.discard(op)
        op.deps = list(deps)
        for r in reads:
            self.readers.setdefault(r, []).append(op)
        for w_ in writes:
            self.last_writer[w_] = op
            self.readers[w_] = []

    def op(self, eng, fn, reads=(), writes=()):
        o = Op(eng, fn)
        self._track(o, reads, writes)
        self.ops[eng].append(o)
        self.all_ops.append(o)
        return o

    def dma(self, eng, out, in_, reads=(), writes=(), group=None):
        cnt = self.dma_groups.get(group, 0) + 1
        self.dma_groups[group] = cnt

        def fn(e, out=out, in_=in_):
            return e.dma_start(out=out, in_=in_)

        o = Op(eng, fn, is_dma=True)
        o.token = (("dma", group), 16 * cnt)
        o.signal = True
        self._track(o, reads, writes)
        self.ops[eng].append(o)
        self.all_ops.append(o)
        return o

    @staticmethod
    def _skip(d, o):
        return (not d.is_dma) and (not o.is_dma) and d.eng == o.eng and d.eng in SAFE_SAME

    def finalize(self, final_waits=()):
        nc = self.nc
        for o in self.all_ops:
            for d in o.deps:
                if d.is_dma or self._skip(d, o):
                    continue
                d.signal = True
        for o in final_waits:
            o.signal = True
        for e in ENGINES:
            c = 0
            for o in self.ops[e]:
                if o.is_dma:
                    continue
                if o.signal:
                    c += 1
                    o.token = (("eng", e), c)
            assert c < 65000, (e, c)
        for g, c in self.dma_groups.items():
            assert 16 * c < 65000, (g, c)
        for e in ENGINES:
            seen = {}
            for o in self.ops[e]:
                need = {}
                for d in o.deps:
                    if self._skip(d, o):
                        continue
                    k, v = d.token
                    if seen.get(k, 0) >= v:
                        continue
                    if need.get(k, 0) < v:
                        need[k] = v
                for k, v in need.items():
                    seen[k] = v
                o.waits = list(need.items())
        sem_keys = [("eng", e) for e in ENGINES
                    if any(o.signal and not o.is_dma for o in self.ops[e])]
        sem_keys += [("dma", g) for g in self.dma_groups]
        with contextlib.ExitStack() as st:
            sems = {}
            for i, k in enumerate(sem_keys):
                sems[k] = st.enter_context(nc.semaphore("s%d" % i))
            block = st.enter_context(nc.Block())
            fin = [o.token for o in final_waits]

            def run(e_name, eng):
                for o in self.ops[e_name]:
                    for k, v in o.waits:
                        eng.wait_ge(sems[k], v)
                    ins = o.fn(eng)
                    if o.signal:
                        ins.then_inc(sems[o.token[0]], 16 if o.is_dma else 1)
                if e_name == "sp":
                    for k, v in fin:
                        eng.wait_ge(sems[k], v)

            @block.tensor
            def _(t):
                run("pe", t)

            @block.scalar
            def _(s):
                run("act", s)

            @block.vector
            def _(v):
                run("dve", v)

            @block.gpsimd
            def _(g):
                run("pool", g)

            @block.sync
            def _(s):
                run("sp", s)


class _Stop(Exception):
    pass


def build_program(S, L, T, NSLOT=10, debug=None):
    assert S % T == 0 and T % 512 == 0
    NT = S // T
    NB = T // 128
    NH = T // 512
    NCK = T // 64
    nc = bass.Bass("TRN2", target_bir_lowering=False)
    x_d = nc.dram_tensor("x", [128, NCH, S], F32, kind="ExternalInput").ap()
    wp_d = nc.dram_tensor("wp", [L, NPANEL, 128, 8, 128], F32, kind="ExternalInput").ap()
    ws_d = nc.dram_tensor("ws", [128, L, 8, 128], F32, kind="ExternalInput").ap()
    bs_d = nc.dram_tensor("bs", [L, 1024], F32, kind="ExternalInput").ap()
    sg_d = nc.dram_tensor("sgu_g", [L, 1024], F32, kind="ExternalInput").ap()
    sb_d = nc.dram_tensor("sgu_b", [L, 1024], F32, kind="ExternalInput").ap()
    pv_d = nc.dram_tensor("pvec", [128, 3 * L * 8 + 8 + L], F32, kind="ExternalInput").ap()
    y_d = nc.dram_tensor("y", [128, NCH, S], F32, kind="ExternalOutput").ap()

    with contextlib.ExitStack() as st:
        def sb(name, shape, dt):
            return st.enter_context(nc.sbuf_tensor(name, shape, dt))

        X = sb("X", [128, NCH, T], F32)
        H = sb("H", [128, NCH, T], BF16)
        R = sb("R", [128, 24, T], BF16)
        Sst = sb("Sst", [128, L, 8, 128], F32)
        SRall = sb("SRall", [128, T // 64, 128], BF16)
        PT2 = sb("PT2", [128, T], BF16)
        KTT2 = sb("KTT2", [128, T], BF16)
        ring = sb("ring", [128, NSLOT, 8, 128], BF16)
        FT = [sb("FT%d" % i, [128, T], F32) for i in range(5)]
        BT = [sb("BT%d" % i, [128, T], BF16) for i in range(8)]
        WS = sb("WS", [128, L, 8, 128], BF16)
        GBC = sb("GBC", [128, 1024], BF16)
        BBC = sb("BBC", [128, 1024], BF16)
        BSR = sb("BSR", [1, 1024], F32)
        M01 = sb("M01", [128, T], F32)
        PV = sb("PV", [128, 3 * L * 8 + 8 + L], F32)
        LB = sb("LB", [128, L, 8], F32)
        LBm1 = sb("LBm1", [128, L, 8], F32)
        OML = sb("OML", [128, L, 8], F32)
        EXPL = sb("EXPL", [128, L, 8], F32)
        small = sb("small", [128, 64], F32)
        EB2 = sb("EB2", [128, 2, NCK], F32)
        identf = sb("identf", [128, 128], F32)
        ident = sb("ident", [128, 128], BF16)
        maskf = sb("maskf", [128, 128], F32)
        onesD = sb("onesD", [128, 128], BF16)
        onesV = sb("onesV", [128, 128], BF16)
        ones1 = sb("ones1", [1, 128], F32)
        epsT = sb("epsT", [128, 1], F32)
        psum = st.enter_context(nc.psum_tensor("psum", [128, 8, 512], F32))

        P = Prog(nc)
        panel_order = []
        JK = BT[5]
        A_ = lambda g: R[:, g, :]
        Bb_ = lambda h: R[:, 8 + h, :]
        VTflats = {8: R[:, 8:16, :].rearrange("p c t -> p (c t)"), 16: R[:, 16:24, :].rearrange("p c t -> p (c t)")}

        def VT(blk, lo=0, hi=1024, p0=0, p1=128, reg=16):
            return VTflats[reg][p0:p1, blk * 1024 + lo: blk * 1024 + hi]

        def vt_res(blk, reg=16):
            a = (blk * 1024) // T
            b = (blk * 1024 + 1023) // T
            return [("R", reg + i) for i in range(a, b + 1)]

        MG_ = lambda m: R[:, 16 + m, :]
        HID_ = lambda j: R[:, j, :]

        def bank(i):
            return psum[:, i, :]

        def pair(q):
            return psum[:, 2 * q:2 * q + NH, :].rearrange("p b f -> p (b f)")

        def pair_res(q):
            return [("b", 2 * q + i) for i in range(NH)]

        qstate = {"q": 0, "s": 0, "c": 0}

        def next_pair():
            q = qstate["q"]
            if qstate.get("hmode"):
                q = q % 2
                qstate["q"] = (q + 1) % 2
            else:
                qstate["q"] = (q + 1) % 3
            return q

        def next_single():
            s_ = qstate["s"]
            qstate["s"] = 1 - s_
            return 6 + s_

        def next_chain():
            c = qstate["c"]
            qstate["c"] = 1 - c
            return 6 + c

        uses = [(l, pi) for _t in range(NT) for l in range(L) for pi in range(NPANEL)]
        wstate = {"next_load": 0, "next_use": 0}

        def load_more():
            while wstate["next_load"] < len(uses) and wstate["next_load"] < wstate["next_use"] + NSLOT - 1:
                u = wstate["next_load"]
                l, pi = uses[u]
                s_ = u % NSLOT
                P.dma("pool", ring[:, s_, :, :], wp_d[l, pi], writes=[("ring", s_)], group=("ring", s_))
                wstate["next_load"] += 1

        def next_panel(l, pi):
            u = wstate["next_use"]
            assert uses[u] == (l, pi), (uses[u], l, pi)
            load_more()
            wstate["next_use"] += 1
            s_ = u % NSLOT
            return s_

        def proj(slot, act, act_res, q, kcs=range(8), first=True, last=True, kofs=0):
            for _ in proj_g(slot, act, act_res, q, kcs, first, last, kofs):
                pass

        def proj_g(slot, act, act_res, q, kcs=range(8), first=True, last=True, kofs=0):
            for hf in range(NH):
                def fn(e, hf=hf):
                    r = None
                    for i, kc in enumerate(kcs):
                        r = e.matmul(psum[:, 2 * q + hf, :], ring[:, slot, kc, :],
                                     act[:, kofs + kc, hf * 512:(hf + 1) * 512],
                                     start=(first and i == 0), stop=(last and i == len(kcs) - 1))
                    return r
                P.op("pe", fn, reads=[("ring", slot)] + act_res, writes=[("b", 2 * q + hf)])
                yield

        P.dma("sp", PV[:], pv_d, writes=["PV"], group="c0")
        P.op("pool", lambda e: e.memset(identf[:], 1.0), writes=["identf"])
        P.op("pool", lambda e: e.affine_select(out=identf[:], in_=identf[:], pattern=[[-1, 128]],
                                               compare_op=ALU.is_equal, fill=0.0, base=0, channel_multiplier=1),
             reads=["identf"], writes=["identf"])
        P.op("dve", lambda e: e.tensor_copy(out=ident[:], in_=identf[:]), reads=["identf"], writes=["ident"])
        P.op("pool", lambda e: e.memset(maskf[:], 1.0), writes=["maskf"])
        P.op("pool", lambda e: e.affine_select(out=maskf[:], in_=maskf[:], pattern=[[1, 128]],
                                               compare_op=ALU.is_ge, fill=0.0, base=0, channel_multiplier=-1),
             reads=["maskf"], writes=["maskf"])
        P.op("dve", lambda e: e.memset(onesD[:], 1.0 / 1024.0), writes=["onesD"])
        P.op("dve", lambda e: e.memset(onesV[:], 1.0 / 128.0), writes=["onesV"])
        P.op("dve", lambda e: e.memset(ones1[:], 1.0), writes=["ones1"])
        P.op("dve", lambda e: e.memset(epsT[:], EPS), writes=["epsT"])
        P.op("dve", lambda e: e.memset(M01[:], 1.0), writes=["M01"])
        P.op("dve", lambda e: e.memset(M01[:].rearrange("p (c k) -> p c k", k=128)[:, :, 0:1], 0.0),
             reads=["M01"], writes=["M01"])
        P.op("dve", lambda e: e.memset(Sst[:], 0.0), writes=[("S", l_, h_) for l_ in range(L) for h_ in range(8)])
        for l in range(L):
            XS = X[:, 0:2, :].rearrange("p c t -> p (c t)")[:, 0:1024]
            P.dma("sp", XS, ws_d[:, l].rearrange("p g t -> p (g t)"), writes=[("X", 0), ("X", 1)], group="c1")
            P.op("dve", lambda e, l=l, XS=XS: e.tensor_tensor(
                out=WS[:, l], in0=XS.rearrange("p (g t) -> p g t", g=8),
                in1=maskf[:].unsqueeze(1).broadcast_to([128, 8, 128]), op=ALU.mult),
                reads=[("X", 0), ("X", 1), "maskf"], writes=["WS"])
        P.op("dve", lambda e: e.memset(BT[6][:], 0.0), writes=[("BT", 6)])
        P.op("dve", lambda e: e.memset(PT2[:], 0.0), writes=["PT2"])
        pv_lb = PV[:, 2 * L * 8:3 * L * 8].rearrange("p (l c) -> p l c", l=L)
        P.op("act", lambda e: e.activation(out=EXPL[:], in_=pv_lb, func=AF.Exp), reads=["PV"], writes=["EXPL"])

        def pl(fn, reads, writes):
            P.op("pool", fn, reads=reads, writes=writes)
        pl(lambda e: e.tensor_copy(out=small[:, 0:8], in_=EXPL[:, 0, :]), ["EXPL"], ["sm0"])
        for l in range(1, L):
            pl(lambda e, l=l: e.tensor_tensor(out=small[:, 0:8], in0=small[:, 0:8], in1=EXPL[:, l, :], op=ALU.add), ["EXPL", "sm0"], ["sm0"])
        P.op("dve", lambda e: e.reciprocal(out=small[:, 8:16], in_=small[:, 0:8]), reads=["sm0"], writes=["sm8"])
        pl(lambda e: e.memset(LB[:, 0, :], 0.0), [], [("LBl", 0)])
        for l in range(1, L):
            pl(lambda e, l=l: e.tensor_tensor(out=small[:, 16:24], in0=EXPL[:, l, :], in1=small[:, 8:16], op=ALU.mult), ["EXPL", "sm8"], ["sm16"])
            pl(lambda e, l=l: e.tensor_tensor(out=LB[:, l, :], in0=LB[:, l - 1, :], in1=small[:, 16:24], op=ALU.add), ["sm16", ("LBl", l - 1)], [("LBl", l)])
        pl(lambda e: e.tensor_scalar(out=LBm1[:], in0=LB[:], scalar1=-1.0, scalar2=None, op0=ALU.add), [("LBl", l) for l in range(L)], ["LBm1"])
        pl(lambda e: e.tensor_scalar(out=OML[:], in0=LBm1[:], scalar1=-1.0, scalar2=None, op0=ALU.mult), ["LBm1"], ["LB"])

        mixg = lambda l: PV[:, l * 8:(l + 1) * 8]
        mlpg = lambda l: PV[:, L * 8 + l * 8: L * 8 + (l + 1) * 8]
        fing = PV[:, 3 * L * 8:3 * L * 8 + 8]
        hng = lambda l: PV[:, 3 * L * 8 + 8 + l: 3 * L * 8 + 8 + l + 1]

        XR = [("X", c) for c in range(8)]
        HR = [("H", c) for c in range(8)]

        def rmsnorm(gains, to_x=False):
            q = next_pair()
            for c in range(8):
                bt = c % 2
                if bt == 0:
                    P.op("act", lambda e, c=c, bt=bt: e.activation(out=BT[bt][:], in_=X[:, c, :], func=AF.Square),
                         reads=[("X", c)], writes=[("BT", bt)])
                else:
                    P.op("pool", lambda e, c=c, bt=bt: e.tensor_tensor(out=BT[bt][:], in0=X[:, c, :], in1=X[:, c, :], op=ALU.mult),
                         reads=[("X", c)], writes=[("BT", bt)])
                for hf in range(NH):
                    P.op("pe", lambda e, c=c, bt=bt, hf=hf: e.matmul(
                        psum[:, 2 * q + hf, :], onesD[:], BT[bt][:, hf * 512:(hf + 1) * 512],
                        start=(c == 0), stop=(c == 7)),
                        reads=[("BT", bt), "onesD"], writes=[("b", 2 * q + hf)])
            RS = FT[4]
            P.op("act", lambda e: e.activation(out=RS[:], in_=pair(q), func=AF.Ln, bias=epsT[:]),
                 reads=["epsT"], writes=pair_res(q) + [("FT", 4)])
            P.op("act", lambda e: e.activation(out=RS[:], in_=RS[:], func=AF.Exp, scale=-0.5),
                 reads=[], writes=[("FT", 4)])
            for c in range(8):
                eng_ = "dve"
                if to_x:
                    P.op(eng_, lambda e, c=c: e.scalar_tensor_tensor(
                        out=X[:, c, :], in0=X[:, c, :], scalar=gains[:, c:c + 1], in1=RS[:],
                        op0=ALU.mult, op1=ALU.mult), reads=[("FT", 4), "PV"], writes=[("X", c)])
                else:
                    P.op(eng_, lambda e, c=c: e.scalar_tensor_tensor(
                        out=H[:, c, :], in0=X[:, c, :], scalar=gains[:, c:c + 1], in1=RS[:],
                        op0=ALU.mult, op1=ALU.mult), reads=[("FT", 4), "PV", ("X", c)], writes=[("H", c)])

        def to_token_major(src_bt, c, reg=16):
            for b0 in range(0, NB, 8):
                nb = min(8, NB - b0)
                sbk = next_single()
                pT = bank(sbk).bitcast(BF16)

                def fn(e, b0=b0, nb=nb, pT=pT):
                    r = None
                    for j in range(nb):
                        r = e.transpose(out=pT[:, j * 128:(j + 1) * 128],
                                        in_=BT[src_bt][:, (b0 + j) * 128:(b0 + j + 1) * 128], identity=ident[:])
                    return r
                P.op("pe", fn, reads=[("BT", src_bt), "ident"], writes=[("b", sbk)])
                res = []
                for j in range(nb):
                    res += vt_res(b0 + j, reg)

                def ev(e, b0=b0, nb=nb, pT=pT):
                    dst = VTflats[reg][:, b0 * 1024:(b0 + nb) * 1024].rearrange("p (j f) -> p j f", f=1024)[:, :, c * 128:(c + 1) * 128]
                    return e.tensor_copy(out=dst, in_=pT[:, 0:nb * 128].rearrange("p (j f) -> p j f", f=128))
                P.op("dve", ev, reads=[], writes=[("b", sbk)] + sorted(set(res)))

        def dump(src_fn):
            for c in range(8):
                P.op("dve", lambda e, c=c: e.tensor_copy(out=X[:, c, :], in_=src_fn(c)),
                     reads=HR + [("R", i) for i in range(24)], writes=[("X", c)])
            raise _Stop()

        def layer(l):
            base = {"pi": 0}

            def panel(key):
                pi = base["pi"]
                base["pi"] += 1
                if len(panel_order) < NPANEL:
                    panel_order.append(key)
                else:
                    assert panel_order[pi] == key, (pi, key, panel_order[pi])
                return next_panel(l, pi)

            P.dma("pool", GBC[:], sg_d[l].partition_broadcast(128), writes=["GBC"], group="gbc")
            P.dma("pool", BBC[:], sb_d[l].partition_broadcast(128), writes=["BBC"], group="gbc")
            P.dma("sp", BSR[:], bs_d[l:l + 1, :], writes=["BSR"], group="bsr")

            rmsnorm(mixg(l))

            if debug == "norm":
                dump(lambda c: H[:, c, :])
            for c in range(8):
                s_ = panel(('v', c))
                q = next_pair()
                proj(s_, H, HR, q)
                bt = c % 2
                P.op("act", lambda e, q=q, bt=bt: e.activation(out=BT[bt][:], in_=pair(q), func=AF.Gelu_apprx_tanh),
                     reads=[], writes=pair_res(q) + [("BT", bt)])
                to_token_major(bt, c, reg=8)
            def zacc(e):
                e.memset(small[:, 24:24 + NB], 0.0)
                return e.memset(small[:, 40:40 + NB], 0.0)
            P.op("dve", zacc, reads=[], writes=[("sm", 24 + b) for b in range(NB)] + [("sm", 40 + b) for b in range(NB)] + ["lnstat", "lnstat2", "lnstat3", "ln_mean", "ln_m2", "ln_e2"])
            for c in range(8):
                s_ = panel(('i', c))
                q = next_pair()
                proj(s_, H, HR, q)
                bt = c % 2
                P.op("act", lambda e, q=q, bt=bt: e.activation(out=BT[bt][:], in_=pair(q), func=AF.Copy),
                     reads=[], writes=pair_res(q) + [("BT", bt)])
                to_token_major(bt, c, reg=16)
                for blk in range(c * NB // 8, (c + 1) * NB // 8):
                    P.op("act", lambda e, blk=blk: e.activation(out=JK[:], in_=VT(blk, reg=8), func=AF.Copy,
                                                              accum_out=small[:, 24 + blk:25 + blk]),
                         reads=vt_res(blk, 8), writes=[("BT", 5), ("sm", 24 + blk)])
                    P.op("act", lambda e, blk=blk: e.activation(out=JK[:], in_=VT(blk, reg=8), func=AF.Square,
                                                              accum_out=small[:, 40 + blk:41 + blk]),
                         reads=vt_res(blk, 8), writes=[("BT", 5), ("sm", 40 + blk)])

            smr = [("sm", 24 + b) for b in range(NB)] + [("sm", 40 + b) for b in range(NB)]
            P.op("pool", lambda e: e.tensor_scalar(out=small[:, 24:24 + NB], in0=small[:, 24:24 + NB], scalar1=1.0 / 1024, scalar2=None, op0=ALU.mult),
                 reads=smr, writes=["ln_mean"])
            P.op("pool", lambda e: e.tensor_tensor(out=small[:, 56:56 + NB], in0=small[:, 24:24 + NB], in1=small[:, 24:24 + NB], op=ALU.mult),
                 reads=["ln_mean"], writes=["ln_m2"])
            P.op("pool", lambda e: e.tensor_scalar(out=small[:, 40:40 + NB], in0=small[:, 40:40 + NB], scalar1=1.0 / 1024, scalar2=None, op0=ALU.mult),
                 reads=smr, writes=["ln_e2"])
            P.op("pool", lambda e: e.tensor_tensor(out=small[:, 40:40 + NB], in0=small[:, 40:40 + NB], in1=small[:, 56:56 + NB], op=ALU.subtract),
                 reads=["ln_e2", "ln_m2"], writes=["ln_e2"])
            P.op("pool", lambda e: e.tensor_scalar(out=small[:, 40:40 + NB], in0=small[:, 40:40 + NB], scalar1=EPS, scalar2=None, op0=ALU.add),
                 reads=["ln_e2"], writes=["ln_e2", "lnstat"])
            P.op("act", lambda e: e.activation(out=small[:, 40:40 + NB], in_=small[:, 40:40 + NB], func=AF.Ln),
                 reads=["lnstat"], writes=["lnstat2"])
            P.op("act", lambda e: e.activation(out=small[:, 40:40 + NB], in_=small[:, 40:40 + NB], func=AF.Exp, scale=-0.5),
                 reads=["lnstat2"], writes=["lnstat3"])
            for blk in range(NB):
                def lnap(e, blk=blk):
                    e.tensor_scalar(out=VT(blk, reg=8), in0=VT(blk, reg=8), scalar1=small[:, 24 + blk:25 + blk],
                                    scalar2=small[:, 40 + blk:41 + blk], op0=ALU.subtract, op1=ALU.mult)
                    e.tensor_tensor(out=VT(blk, reg=8), in0=VT(blk, reg=8), in1=GBC[:], op=ALU.mult)
                    return e.tensor_tensor(out=VT(blk, reg=8), in0=VT(blk, reg=8), in1=BBC[:], op=ALU.add)
                P.op("dve", lnap, reads=["lnstat3", "ln_mean", "GBC", "BBC"], writes=vt_res(blk, 8))

            def phaseU():
                for g in range(8):
                    s_ = panel(('u', g))
                    q = next_pair()
                    for _ in proj_g(s_, H, HR, q):
                        yield
                    bt = g % 2
                    P.op("act", lambda e, q=q, bt=bt: e.activation(out=BT[bt][:], in_=pair(q), func=AF.Gelu_apprx_tanh),
                         reads=[], writes=pair_res(q) + [("BT", bt)])
                    yield
                    qm = next_pair()
                    for hf in range(NH):
                        def mix(e, hf=hf, g=g, qm=qm):
                            r = None
                            for j in range(4):
                                blk = hf * 4 + j
                                e.matmul(psum[:, 2 * qm + hf, j * 128:(j + 1) * 128], ones1[0:1, :],
                                         BSR[0:1, g * 128:(g + 1) * 128], start=(j == 0), stop=False)
                                r = e.matmul(psum[:, 2 * qm + hf, j * 128:(j + 1) * 128],
                                             VT(blk, g * 128, (g + 1) * 128, reg=8), WS[:, l, g, :],
                                             start=False, stop=(j == 3))
                            return r
                        res = []
                        for j in range(4):
                            res += vt_res(hf * 4 + j, 8)
                        P.op("pe", mix, reads=sorted(set(res)) + ["WS", "BSR", "ones1"], writes=[("b", 2 * qm + hf)])
                        yield
                    P.op("dve", lambda e, g=g, qm=qm, bt=bt: e.tensor_tensor(out=A_(g), in0=pair(qm), in1=BT[bt][:], op=ALU.mult),
                         reads=[("BT", bt)], writes=pair_res(qm) + [("R", g)])
                    yield

            SG, LF, BC, DD, RS = FT[0], FT[1], FT[2], FT[3], FT[4]
            KT, OSQ = BT[2], BT[5]
            QTs, GSs, PTss, KTTs = [BT[3], BT[0]], [BT[4], BT[1]], [BT[6], PT2], [BT[7], KTT2]
            QTr, GSr, PTr, KTr = [("BT", 3), ("BT", 0)], [("BT", 4), ("BT", 1)], [("BT", 6), "PT2"], [("BT", 7), "KTT2"]
            EBr = [("EB", 0), ("EB", 1)]

            def prepA(hh):
                p = hh % 2
                QT, GS, PTs, KTT = QTs[p], GSs[p], PTss[p], KTTs[p]
                EB1 = EB2[:, p, 0:NB]
                EBB = EB2[:, p, NB:2 * NB]
                sf_ = panel(('f', hh))
                qf = next_pair()
                for _ in proj_g(sf_, H, HR, qf):
                    yield
                P.op("act", lambda e, qf=qf: e.activation(out=SG[:], in_=pair(qf), func=AF.Sigmoid),
                     reads=[], writes=pair_res(qf) + [("FT", 0)])
                yield
                P.op("act", lambda e, hh=hh: e.activation(out=LF[:], in_=SG[:], func=AF.Ln,
                                                          scale=OML[:, l, hh:hh + 1], bias=LB[:, l, hh:hh + 1]),
                     reads=[("FT", 0), "LB"], writes=[("FT", 1)])
                yield
                P.op("dve", lambda e: e.tensor_tensor_scan(out=BC[:], data0=M01[:], data1=LF[:], initial=0.0,
                                                           op0=ALU.mult, op1=ALU.add),
                     reads=[("FT", 1), "M01"], writes=[("FT", 2)])
                yield
                BCv = BC[:].rearrange("p (c k) -> p c k", k=128)
                P.op("dve", lambda e, BCv=BCv: e.tensor_tensor(
                    out=DD[:].rearrange("p (c k) -> p c k", k=128), in0=BCv[:, :, 63:64].broadcast_to([128, NB, 128]),
                    in1=BCv, op=ALU.subtract), reads=[("FT", 2)], writes=[("FT", 3)])
                yield
                P.op("act", lambda e, BCv=BCv, EB1=EB1: e.activation(out=EB1.unsqueeze(2), in_=BCv[:, :, 63:64], func=AF.Exp),
                     reads=[("FT", 2)], writes=[EBr[p]])
                P.op("act", lambda e, BCv=BCv, EBB=EBB: e.activation(out=EBB.unsqueeze(2), in_=BCv[:, :, 127:128], func=AF.Exp),
                     reads=[("FT", 2)], writes=[EBr[p]])
                P.op("dve", lambda e, BCv=BCv: e.tensor_tensor(
                    out=LF[:].rearrange("p (c k) -> p c k", k=128), in0=BCv[:, :, 127:128].broadcast_to([128, NB, 128]),
                    in1=BCv, op=ALU.subtract), reads=[("FT", 2)], writes=[("FT", 1)])
                yield
                P.op("dve", lambda e, hh=hh: e.tensor_scalar(out=SG[:], in0=SG[:], scalar1=-1.0, scalar2=LBm1[:, l, hh:hh + 1],
                                                             op0=ALU.add, op1=ALU.mult),
                     reads=["LB"], writes=[("FT", 0)])
                yield
                P.op("act", lambda e: e.activation(out=LF[:], in_=LF[:], func=AF.Exp),
                     reads=[], writes=[("FT", 1)])
                yield
                P.op("dve", lambda e: e.tensor_tensor(out=KT[:], in0=SG[:], in1=LF[:], op=ALU.mult),
                     reads=[("FT", 0), ("FT", 1)], writes=[("BT", 2)])
                yield
                sbk = next_single()
                pT = bank(sbk).bitcast(BF16)

                def ktr(e, pT=pT):
                    r = None
                    for j in range(NB):
                        r = e.transpose(out=pT[:, j * 128:(j + 1) * 128],
                                        in_=KT[:, j * 128:(j + 1) * 128], identity=ident[:])
                    return r
                P.op("pe", ktr, reads=[("BT", 2), "ident"], writes=[("b", sbk)])
                P.op("dve", lambda e, pT=pT, KTT=KTT: e.tensor_copy(out=KTT[:, 0:NB * 128], in_=pT[:, 0:NB * 128]),
                     reads=[], writes=[("b", sbk), KTr[p]])
                yield
                P.op("act", lambda e: e.activation(out=LF[:], in_=DD[:], func=AF.Exp),
                     reads=[("FT", 3)], writes=[("FT", 1)])
                yield
                P.op("dve", lambda e: e.tensor_tensor(out=KT[:], in0=SG[:], in1=LF[:], op=ALU.mult),
                     reads=[("FT", 0), ("FT", 1)], writes=[("BT", 2)])
                yield
                P.op("act", lambda e: e.activation(out=DD[:], in_=DD[:], func=AF.Exp, scale=-1.0),
                     reads=[], writes=[("FT", 3)])
                yield
                sq_ = panel(('q', hh))
                qq = next_pair()
                for _ in proj_g(sq_, H, HR, qq):
                    yield
                P.op("act", lambda e, qq=qq: e.activation(out=BC[:], in_=pair(qq), func=AF.Silu),
                     reads=[EBr[p]], writes=pair_res(qq) + [("FT", 2)])
                yield
                P.op("dve", lambda e, QT=QT: e.tensor_tensor(out=QT[:], in0=BC[:], in1=DD[:], op=ALU.mult),
                     reads=[("FT", 2), ("FT", 3)], writes=[QTr[p]])
                yield
                sg_ = panel(('g', hh))
                qg = next_pair()
                for _ in proj_g(sg_, H, HR, qg):
                    yield
                P.op("act", lambda e, qg=qg, GS=GS: e.activation(out=GS[:], in_=pair(qg), func=AF.Silu),
                     reads=[], writes=pair_res(qg) + [GSr[p]])
                yield
                qs = next_pair()
                for hf in range(NH):
                    def sc(e, hf=hf, qs=qs, QT=QT):
                        r = None
                        for j in range(4):
                            blk = hf * 4 + j
                            e.matmul(psum[:, 2 * qs + hf, j * 128 + 64:(j + 1) * 128],
                                     KT[:, blk * 128:(blk + 1) * 128], QT[:, blk * 128 + 64:(blk + 1) * 128],
                                     start=(j == 0), stop=False)
                            r = e.matmul(psum[0:64, 2 * qs + hf, j * 128:j * 128 + 64],
                                         KT[:, blk * 128:blk * 128 + 64], QT[:, blk * 128:blk * 128 + 64],
                                         start=False, stop=(j == 3))
                        return r
                    P.op("pe", sc, reads=[("BT", 2), QTr[p]], writes=[("b", 2 * qs + hf)])
                    yield

                def mk(e, qs=qs, PTs=PTs):
                    pv = pair(qs).rearrange("p (j t) -> p j t", t=128)
                    tv = PTs[:].rearrange("p (j t) -> p j t", t=128)
                    e.tensor_tensor(out=tv[0:64], in0=pv[0:64], in1=maskf[0:64, :].unsqueeze(1).broadcast_to([64, NB, 128]),
                                    op=ALU.mult)
                    return e.tensor_tensor(out=tv[64:128, :, 64:128], in0=pv[64:128, :, 64:128],
                                           in1=maskf[64:128, 64:128].unsqueeze(1).broadcast_to([64, NB, 64]), op=ALU.mult)
                P.op("dve", mk, reads=["maskf"], writes=pair_res(qs) + [PTr[p]])
                yield

            def chainB(hh):
                if OLD_CHAIN:
                    yield from chainB_old(hh)
                else:
                    yield from chainB_new(hh)

            def chainB_old(hh):
                p = hh % 2
                QT, GS, PTs, KTT = QTs[p], GSs[p], PTss[p], KTTs[p]
                Sv = Sst[:, l, hh, :]
                qo = 2
                for blk in range(NB):
                    hf, j = blk // 4, blk % 4
                    par = blk % 2
                    vals = lambda p0, p1, blk=blk, hh=hh: VT(blk, hh * 128, (hh + 1) * 128, p0, p1)
                    P.op("pe", lambda e, blk=blk, hf=hf, j=j, vals=vals, PTs=PTs: e.matmul(
                        psum[:, 2 * qo + hf, j * 128:(j + 1) * 128], vals(0, 128), PTs[:, blk * 128:(blk + 1) * 128],
                        start=(j == 0), stop=False),
                        reads=vt_res(blk) + [PTr[p]], writes=[("b", 2 * qo + hf)])
                    P.op("act", lambda e, blk=blk, par=par, Sv=Sv, p=p: e.activation(out=SRall[:, par, :], in_=Sv, func=AF.Copy, scale=EB2[:, p, blk:blk + 1]),
                         reads=[EBr[p], ("S", l, hh)], writes=[("SR", par)])
                    P.op("pe", lambda e, blk=blk, hf=hf, j=j, par=par, QT=QT: e.matmul(
                        psum[:, 2 * qo + hf, j * 128:(j + 1) * 128], SRall[:, par, :],
                        QT[:, blk * 128:(blk + 1) * 128], start=False, stop=(j == 3)),
                        reads=[("SR", par), QTr[p]], writes=[("b", 2 * qo + hf)])
                    cb = next_chain()
                    P.op("pe", lambda e, blk=blk, cb=cb, vals=vals, KTT=KTT: e.matmul(
                        psum[:, cb, 0:128], KTT[:, blk * 128:(blk + 1) * 128], vals(0, 128),
                        start=True, stop=True),
                        reads=vt_res(blk) + [KTr[p]], writes=[("b", cb)])
                    P.op("dve", lambda e, cb=cb, blk=blk, Sv=Sv, p=p: e.scalar_tensor_tensor(
                        out=Sv, in0=Sv, scalar=EB2[:, p, NB + blk:NB + blk + 1], in1=psum[:, cb, 0:128],
                        op0=ALU.mult, op1=ALU.add),
                        reads=[EBr[p]], writes=[("b", cb), ("S", l, hh)])
                    yield
                yield from chain_end(hh)

            def chainB_new(hh):
                p = hh % 2
                QT, GS, PTs, KTT = QTs[p], GSs[p], PTss[p], KTTs[p]
                Sv = Sst[:, l, hh, :]
                qo = 2
                vals = lambda blk, p0, p1, hh=hh: VT(blk, hh * 128, (hh + 1) * 128, p0, p1)
                P.op("dve", lambda e, Sv=Sv, p=p: e.tensor_scalar(out=SRall[:, 0, :], in0=Sv, scalar1=EB2[:, p, 0:1], scalar2=None, op0=ALU.mult),
                     reads=[EBr[p], ("S", l, hh)], writes=[("SRs", 0)])
                yield
                for hf in range(NH):
                    def intra(e, hf=hf, PTs=PTs, vals=vals):
                        r = None
                        for j in range(4):
                            blk = hf * 4 + j
                            r = e.matmul(psum[:, 2 * qo + hf, j * 128:(j + 1) * 128], vals(blk, 0, 128),
                                         PTs[:, blk * 128:(blk + 1) * 128], start=(j == 0), stop=False)
                        return r
                    res = []
                    for j in range(4):
                        res += vt_res(hf * 4 + j)
                    P.op("pe", intra, reads=sorted(set(res)) + [PTr[p]], writes=[("b", 2 * qo + hf)])
                    yield
                for g in range(NCK // 4):
                    cb = 6 + g % 2

                    def ugrp(e, g=g, cb=cb, KTT=KTT, vals=vals):
                        r = None
                        for i in range(4):
                            c = 4 * g + i
                            blk, ck = c // 2, c % 2
                            p0 = ck * 64
                            r = e.matmul(psum[:, cb, i * 128:(i + 1) * 128], KTT[p0:p0 + 64, blk * 128:(blk + 1) * 128],
                                         vals(blk, p0, p0 + 64), start=True, stop=True)
                        return r
                    P.op("pe", ugrp, reads=vt_res(2 * g) + vt_res(2 * g + 1) + [KTr[p]], writes=[("b", cb)])
                    yield

                    def scan(e, g=g, cb=cb, Sv=Sv, p=p):
                        r = None
                        for i in range(4):
                            c = 4 * g + i
                            for hv in range(2):
                                r = e.scalar_tensor_tensor(out=Sv[:, hv * 64:hv * 64 + 64], in0=Sv[:, hv * 64:hv * 64 + 64],
                                                           scalar=EB2[:, p, c:c + 1],
                                                           in1=psum[:, cb, i * 128 + hv * 64:i * 128 + hv * 64 + 64],
                                                           op0=ALU.mult, op1=ALU.add)
                            if c + 1 < NCK:
                                for hv in range(2):
                                    r = e.tensor_scalar(out=SRall[:, c + 1, hv * 64:hv * 64 + 64], in0=Sv[:, hv * 64:hv * 64 + 64],
                                                        scalar1=EB2[:, p, c + 1:c + 2], scalar2=None, op0=ALU.mult)
                        return r
                    P.op("dve", scan, reads=[EBr[p]], writes=[("b", cb), ("S", l, hh), ("SRs", g + 1)])
                    yield

                    def inter(e, g=g, QT=QT):
                        r = None
                        for i in range(4):
                            c = 4 * g + i
                            blk, ck = c // 2, c % 2
                            hf, j = blk // 4, blk % 4
                            r = e.matmul(psum[:, 2 * qo + hf, j * 128 + ck * 64:j * 128 + ck * 64 + 64], SRall[:, c, :],
                                         QT[:, blk * 128 + ck * 64:blk * 128 + ck * 64 + 64], start=False, stop=(i == 3))
                        return r
                    P.op("pe", inter, reads=[("SRs", g), ("SRs", g + 1), QTr[p]], writes=[("b", 2 * qo + (2 * g) // 4)])
                    yield
                yield from chain_end(hh)

            def chain_end(hh):
                p = hh % 2
                GS = GSs[p]
                qo = 2
                P.op("act", lambda e: e.activation(out=OSQ[:], in_=pair(qo), func=AF.Square),
                     reads=[], writes=pair_res(qo) + [("BT", 5)])
                yield
                qn = next_pair()
                for hf in range(NH):
                    P.op("pe", lambda e, hf=hf, qn=qn: e.matmul(psum[:, 2 * qn + hf, :], onesV[:],
                                                               OSQ[:, hf * 512:(hf + 1) * 512], start=True, stop=True),
                         reads=[("BT", 5), "onesV"], writes=[("b", 2 * qn + hf)])
                yield
                P.op("act", lambda e, qn=qn: e.activation(out=RS[:], in_=pair(qn), func=AF.Ln, bias=epsT[:]),
                     reads=["epsT"], writes=pair_res(qn) + [("FT", 4)])
                P.op("act", lambda e: e.activation(out=RS[:], in_=RS[:], func=AF.Exp, scale=-0.5),
                     reads=[], writes=[("FT", 4)])
                yield
                P.op("dve", lambda e: e.scalar_tensor_tensor(
                    out=RS[:], in0=pair(qo), scalar=hng(l), in1=RS[:], op0=ALU.mult, op1=ALU.mult),
                    reads=["PV"], writes=pair_res(qo) + [("FT", 4)])
                P.op("dve", lambda e, hh=hh, GS=GS: e.tensor_tensor(out=Bb_(hh), in0=RS[:], in1=GS[:], op=ALU.mult),
                     reads=[("FT", 4), GSr[p]], writes=[("R", 8 + hh)])
                yield

            gU = phaseU()
            gA0 = prepA(0)
            for _ in gU:
                next(gA0, None)
            for _ in gA0:
                pass
            qstate["hmode"] = True
            for hh in range(8):
                gB = chainB(hh)
                gA = prepA(hh + 1) if hh < 7 else iter(())
                if INTERLEAVE:
                    for _ in gB:
                        next(gA, None)
                        next(gA, None)
                    for _ in gA:
                        pass
                else:
                    for _ in gB:
                        pass
                    for _ in gA:
                        pass
            qstate["hmode"] = False

            if debug == "b":
                dump(lambda c: R[:, 8 + c, :])
            AR = [("R", g) for g in range(8)]
            BR = [("R", 8 + g) for g in range(8)]
            for m in range(8):
                sga = panel(('ga', m))
                q1 = next_pair()
                proj(sga, H, HR, q1)
                P.op("act", lambda e, q1=q1: e.activation(out=FT[0][:], in_=pair(q1), func=AF.Sigmoid),
                     reads=[], writes=pair_res(q1) + [("FT", 0)])
                sgb = panel(('gb', m))
                q2 = next_pair()
                proj(sgb, H, HR, q2)
                P.op("act", lambda e, q2=q2: e.activation(out=FT[1][:], in_=pair(q2), func=AF.Sigmoid),
                     reads=[], writes=pair_res(q2) + [("FT", 1)])
                sa = panel(('pa', m))
                q3 = next_pair()
                proj(sa, R, AR, q3)
                P.op("dve", lambda e, q3=q3: e.tensor_tensor(out=FT[0][:], in0=pair(q3), in1=FT[0][:], op=ALU.mult),
                     reads=[], writes=pair_res(q3) + [("FT", 0)])
                sbp = panel(('pb', m))
                q4 = next_pair()
                proj(sbp, R, BR, q4, kofs=8)
                P.op("dve", lambda e, q4=q4: e.tensor_tensor(out=FT[1][:], in0=pair(q4), in1=FT[1][:], op=ALU.mult),
                     reads=[], writes=pair_res(q4) + [("FT", 1)])
                P.op("dve", lambda e, m=m: e.tensor_tensor(out=MG_(m), in0=FT[0][:], in1=FT[1][:], op=ALU.add),
                     reads=[("FT", 0), ("FT", 1)], writes=[("R", 16 + m)])
            if debug == "merged":
                dump(lambda c: R[:, 16 + c, :])
            MR = [("R", 16 + g) for g in range(8)]
            for m in range(8):
                s_ = panel(('o', m))
                q = next_pair()
                proj(s_, R, MR, q, kofs=16)
                P.op("dve", lambda e, q=q, m=m: e.tensor_tensor(out=X[:, m, :], in0=pair(q), in1=X[:, m, :], op=ALU.add),
                     reads=[], writes=pair_res(q) + [("X", m)])

            if debug == "mixer":
                raise _Stop()
            rmsnorm(mlpg(l))
            for half in range(2):
                for j in range(16):
                    s_ = panel(('up', half * 16 + j))
                    q = next_pair()
                    proj(s_, H, HR, q)
                    P.op("act", lambda e, q=q, j=j: e.activation(out=HID_(j), in_=pair(q), func=AF.Relu),
                         reads=[], writes=pair_res(q) + [("R", j)])
                    P.op("dve", lambda e, j=j: e.tensor_tensor(out=HID_(j), in0=HID_(j), in1=HID_(j), op=ALU.mult),
                         reads=[], writes=[("R", j)])
                HIDR = [("R", j) for j in range(16)]
                for m in range(8):
                    q = next_pair()
                    s0 = panel(('dn', half, 0, m))
                    proj(s0, R, HIDR, q, first=True, last=False, kofs=0)
                    s1 = panel(('dn', half, 1, m))
                    proj(s1, R, HIDR, q, first=False, last=True, kofs=8)
                    P.op("dve", lambda e, q=q, m=m: e.tensor_tensor(out=X[:, m, :], in0=pair(q), in1=X[:, m, :], op=ALU.add),
                         reads=[], writes=pair_res(q) + [("X", m)])
            assert base["pi"] == NPANEL

        last = None
        for t in range(NT):
            for c in range(8):
                P.dma("sp", X[:, c, :], x_d[:, c, t * T:(t + 1) * T], writes=[("X", c)], group=("xin", c))
            try:
                for l in range(L):
                    layer(l)
                rmsnorm(fing, to_x=True)
            except _Stop:
                pass
            lasts = []
            for c in range(8):
                lasts.append(P.dma("sp", y_d[:, c, t * T:(t + 1) * T], X[:, c, :], reads=[("X", c)], group=("yout", c)))
        P.finalize(final_waits=lasts)
    nc._panel_order = list(panel_order)
    return nc


def _panelize(w):
    k, n = w.shape
    assert k == 1024
    return np.ascontiguousarray(w.reshape(8, 128, n // 128, 128).transpose(2, 1, 0, 3))


def prep_weights(order, w_in, w_branch_a, w_branch_b, w_out, w_mlp_up, w_mlp_down):
    L = w_in.shape[0]
    out = np.empty((L, NPANEL, 128, 8, 128), np.float32)
    base = {'u': 0, 'v': 8, 'q': 16, 'f': 24, 'i': 32, 'g': 40, 'ga': 48, 'gb': 56}
    assert len(order) == NPANEL and len(set(order)) == NPANEL
    for l in range(L):
        pin = _panelize(w_in[l])
        pa = _panelize(w_branch_a[l])
        pb = _panelize(w_branch_b[l])
        po = _panelize(w_out[l])
        pu = _panelize(w_mlp_up[l])
        wd = w_mlp_down[l]
        pd = [_panelize(wd[i * 1024:(i + 1) * 1024]) for i in range(4)]
        for i, key in enumerate(order):
            k0 = key[0]
            if k0 in base:
                out[l, i] = pin[base[k0] + key[1]]
            elif k0 == 'pa':
                out[l, i] = pa[key[1]]
            elif k0 == 'pb':
                out[l, i] = pb[key[1]]
            elif k0 == 'o':
                out[l, i] = po[key[1]]
            elif k0 == 'up':
                out[l, i] = pu[key[1]]
            elif k0 == 'dn':
                out[l, i] = pd[key[1] * 2 + key[2]][key[3]]
            else:
                raise KeyError(key)
    return out


def _pp(v):
    lead = v.shape[:-1]
    a = v.reshape(*lead, 8, 128)
    return np.moveaxis(a, -1, 0)


_CACHE = {}


def kernel(x, mix_norm_g, w_in, sgu_norm_g, sgu_norm_b, w_spatial, b_spatial, lower_bounds,
           hgrn_norm_g, w_branch_a, w_branch_b, w_out, mlp_norm_g, w_mlp_up, w_mlp_down,
           final_norm_g, T=1024, debug=None):
    x = np.asarray(x, np.float32)
    B, S, _ = x.shape
    L = int(np.asarray(w_in).shape[0])
    f = lambda a: np.asarray(a, np.float32)
    ws = np.ascontiguousarray(f(w_spatial).transpose(3, 0, 1, 2))
    bs = np.ascontiguousarray(f(b_spatial).reshape(L, 1024))
    pvec = np.concatenate([
        _pp(f(mix_norm_g)).reshape(128, L * 8),
        _pp(f(mlp_norm_g)).reshape(128, L * 8),
        _pp(f(lower_bounds)).reshape(128, L * 8),
        _pp(f(final_norm_g)).reshape(128, 8),
        np.ascontiguousarray(f(hgrn_norm_g).T),
    ], axis=1).astype(np.float32)
    pvec = np.ascontiguousarray(pvec)
    key = (S, L, T, debug)
    if key not in _CACHE:
        _CACHE[key] = build_program(S, L, T, debug=debug)
    nc = _CACHE[key]
    wp = prep_weights(nc._panel_order, f(w_in), f(w_branch_a), f(w_branch_b), f(w_out), f(w_mlp_up), f(w_mlp_down))
    in_maps = []
    for b in range(B):
        xb = np.ascontiguousarray(x[b].T.reshape(8, 128, S).transpose(1, 0, 2))
        in_maps.append({"x": xb, "wp": wp, "ws": ws, "bs": bs, "sgu_g": f(sgu_norm_g), "sgu_b": f(sgu_norm_b),
                        "pvec": pvec})
    res = run_bass_kernel_spmd(nc, in_maps, core_ids=list(range(B)))
    out = np.empty((B, S, D), np.float32)
    for b in range(B):
        yb = res.results[b]["y"]
        out[b] = yb.transpose(2, 1, 0).reshape(S, D)
    return out
```

```python
import contextlib
import numpy as np
import concourse.bass as bass
import concourse.mybir as mybir
from concourse.bass_utils import run_bass_kernel_spmd

F32 = mybir.dt.float32
BF16 = mybir.dt.bfloat16
AF = mybir.ActivationFunctionType
ALU = mybir.AluOpType

D = 1024
NCH = 8
DFF = 4096
EPS = 1e-6
NPANEL = 152
INTERLEAVE = True
OLD_CHAIN = True

ENGINES = ("pe", "act", "dve", "pool", "sp")
SAFE_SAME = {"pe", "dve"}


class Op:
    __slots__ = ("eng", "fn", "deps", "signal", "token", "is_dma", "waits")

    def __init__(self, eng, fn, is_dma=False):
        self.eng = eng
        self.fn = fn
        self.deps = []
        self.signal = False
        self.token = None
        self.is_dma = is_dma
        self.waits = []


class Prog:
    def __init__(self, nc):
        self.nc = nc
        self.ops = {e: [] for e in ENGINES}
        self.last_writer = {}
        self.readers = {}
        self.dma_groups = {}
        self.all_ops = []

    def _track(self, op, reads, writes):
        deps = set()
        for r in reads:
            w = self.last_writer.get(r)
            if w is not None:
                deps.add(w)
        for w_ in writes:
            w = self.last_writer.get(w_)
            if w is not None:
                deps.add(w)
            for rd in self.readers.get(w_, ()):
                deps.add(rd)
        deps.discard(op)
        op.deps = list(deps)
        for r in reads:
            self.readers.setdefault(r, []).append(op)
        for w_ in writes:
            self.last_writer[w_] = op
            self.readers[w_] = []

    def op(self, eng, fn, reads=(), writes=()):
        o = Op(eng, fn)
        self._track(o, reads, writes)
        self.ops[eng].append(o)
        self.all_ops.append(o)
        return o

    def dma(self, eng, out, in_, reads=(), writes=(), group=None):
        cnt = self.dma_groups.get(group, 0) + 1
        self.dma_groups[group] = cnt

        def fn(e, out=out, in_=in_):
            return e.dma_start(out=out, in_=in_)

        o = Op(eng, fn, is_dma=True)
        o.token = (("dma", group), 16 * cnt)
        o.signal = True
        self._track(o, reads, writes)
        self.ops[eng].append(o)
        self.all_ops.append(o)
        return o

    @staticmethod
    def _skip(d, o):
        return (not d.is_dma) and (not o.is_dma) and d.eng == o.eng and d.eng in SAFE_SAME

    def finalize(self, final_waits=()):
        nc = self.nc
        for o in self.all_ops:
            for d in o.deps:
                if d.is_dma or self._skip(d, o):
                    continue
                d.signal = True
        for o in final_waits:
            o.signal = True
        for e in ENGINES:
            c = 0
            for o in self.ops[e]:
                if o.is_dma:
                    continue
                if o.signal:
                    c += 1
                    o.token = (("eng", e), c)
            assert c < 65000, (e, c)
        for g, c in self.dma_groups.items():
            assert 16 * c < 65000, (g, c)
        for e in ENGINES:
            seen = {}
            for o in self.ops[e]:
                need = {}
                for d in o.deps:
                    if self._skip(d, o):
                        continue
                    k, v = d.token
                    if seen.get(k, 0) >= v:
                        continue
                    if need.get(k, 0) < v:
                        need[k] = v
                for k, v in need.items():
                    seen[k] = v
                o.waits = list(need.items())
        sem_keys = [("eng", e) for e in ENGINES
                    if any(o.signal and not o.is_dma for o in self.ops[e])]
        sem_keys += [("dma", g) for g in self.dma_groups]
        with contextlib.ExitStack() as st:
            sems = {}
            for i, k in enumerate(sem_keys):
                sems[k] = st.enter_context(nc.semaphore("s%d" % i))
            block = st.enter_context(nc.Block())
            fin = [o.token for o in final_waits]

            def run(e_name, eng):
                for o in self.ops[e_name]:
                    for k, v in o.waits:
                        eng.wait_ge(sems[k], v)
                    ins = o.fn(eng)
                    if o.signal:
                        ins.then_inc(sems[o.token[0]], 16 if o.is_dma else 1)
                if e_name == "sp":
                    for k, v in fin:
                        eng.wait_ge(sems[k], v)

            @block.tensor
            def _(t):
                run("pe", t)

            @block.scalar
            def _(s):
                run("act", s)

            @block.vector
            def _(v):
                run("dve", v)

            @block.gpsimd
            def _(g):
                run("pool", g)

            @block.sync
            def _(s):
                run("sp", s)


class _Stop(Exception):
    pass


def build_program(S, L, T, NSLOT=10, debug=None):
    assert S % T == 0 and T % 512 == 0
    NT = S // T
    NB = T // 128
    NH = T // 512
    NCK = T // 64
    nc = bass.Bass("TRN2", target_bir_lowering=False)
    x_d = nc.dram_tensor("x", [128, NCH, S], F32, kind="ExternalInput").ap()
    wp_d = nc.dram_tensor("wp", [L, NPANEL, 128, 8, 128], F32, kind="ExternalInput").ap()
    ws_d = nc.dram_tensor("ws", [128, L, 8, 128], F32, kind="ExternalInput").ap()
    bs_d = nc.dram_tensor("bs", [L, 1024], F32, kind="ExternalInput").ap()
    sg_d = nc.dram_tensor("sgu_g", [L, 1024], F32, kind="ExternalInput").ap()
    sb_d = nc.dram_tensor("sgu_b", [L, 1024], F32, kind="ExternalInput").ap()
    pv_d = nc.dram_tensor("pvec", [128, 3 * L * 8 + 8 + L], F32, kind="ExternalInput").ap()
    y_d = nc.dram_tensor("y", [128, NCH, S], F32, kind="ExternalOutput").ap()

    with contextlib.ExitStack() as st:
        def sb(name, shape, dt):
            return st.enter_context(nc.sbuf_tensor(name, shape, dt))

        X = sb("X", [128, NCH, T], F32)
        H = sb("H", [128, NCH, T], BF16)
        R = sb("R", [128, 24, T], BF16)
        Sst = sb("Sst", [128, L, 8, 128], F32)
        SRall = sb("SRall", [128, T // 64, 128], BF16)
        PT2 = sb("PT2", [128, T], BF16)
        KTT2 = sb("KTT2", [128, T], BF16)
        ring = sb("ring", [128, NSLOT, 8, 128], BF16)
        FT = [sb("FT%d" % i, [128, T], F32) for i in range(5)]
        BT = [sb("BT%d" % i, [128, T], BF16) for i in range(8)]
        WS = sb("WS", [128, L, 8, 128], BF16)
        GBC = sb("GBC", [128, 1024], BF16)
        BBC = sb("BBC", [128, 1024], BF16)
        BSR = sb("BSR", [1, 1024], F32)
        M01 = sb("M01", [128, T], F32)
        PV = sb("PV", [128, 3 * L * 8 + 8 + L], F32)
        LB = sb("LB", [128, L, 8], F32)
        LBm1 = sb("LBm1", [128, L, 8], F32)
        OML = sb("OML", [128, L, 8], F32)
        EXPL = sb("EXPL", [128, L, 8], F32)
        small = sb("small", [128, 64], F32)
        EB2 = sb("EB2", [128, 2, NCK], F32)
        identf = sb("identf", [128, 128], F32)
        ident = sb("ident", [128, 128], BF16)
        maskf = sb("maskf", [128, 128], F32)
        onesD = sb("onesD", [128, 128], BF16)
        onesV = sb("onesV", [128, 128], BF16)
        ones1 = sb("ones1", [1, 128], F32)
        epsT = sb("epsT", [128, 1], F32)
        psum = st.enter_context(nc.psum_tensor("psum", [128, 8, 512], F32))

        P = Prog(nc)
        panel_order = []
        JK = BT[5]
        A_ = lambda g: R[:, g, :]
        Bb_ = lambda h: R[:, 8 + h, :]
        VTflats = {8: R[:, 8:16, :].rearrange("p c t -> p (c t)"), 16: R[:, 16:24, :].rearrange("p c t -> p (c t)")}

        def VT(blk, lo=0, hi=1024, p0=0, p1=128, reg=16):
            return VTflats[reg][p0:p1, blk * 1024 + lo: blk * 1024 + hi]

        def vt_res(blk, reg=16):
            a = (blk * 1024) // T
            b = (blk * 1024 + 1023) // T
            return [("R", reg + i) for i in range(a, b + 1)]

        MG_ = lambda m: R[:, 16 + m, :]
        HID_ = lambda j: R[:, j, :]

        def bank(i):
            return psum[:, i, :]

        def pair(q):
            return psum[:, 2 * q:2 * q + NH, :].rearrange("p b f -> p (b f)")

        def pair_res(q):
            return [("b", 2 * q + i) for i in range(NH)]

        qstate = {"q": 0, "s": 0, "c": 0}

        def next_pair():
            q = qstate["q"]
            if qstate.get("hmode"):
                q = q % 2
                qstate["q"] = (q + 1) % 2
            else:
                qstate["q"] = (q + 1) % 3
            return q

        def next_single():
            s_ = qstate["s"]
            qstate["s"] = 1 - s_
            return 6 + s_

        def next_chain():
            c = qstate["c"]
            qstate["c"] = 1 - c
            return 6 + c

        uses = [(l, pi) for _t in range(NT) for l in range(L) for pi in range(NPANEL)]
        wstate = {"next_load": 0, "next_use": 0}

        def load_more():
            while wstate["next_load"] < len(uses) and wstate["next_load"] < wstate["next_use"] + NSLOT - 1:
                u = wstate["next_load"]
                l, pi = uses[u]
                s_ = u % NSLOT
                P.dma("pool", ring[:, s_, :, :], wp_d[l, pi], writes=[("ring", s_)], group=("ring", s_))
                wstate["next_load"] += 1

        def next_panel(l, pi):
            u = wstate["next_use"]
            assert uses[u] == (l, pi), (uses[u], l, pi)
            load_more()
            wstate["next_use"] += 1
            s_ = u % NSLOT
            return s_

        def proj(slot, act, act_res, q, kcs=range(8), first=True, last=True, kofs=0):
            for _ in proj_g(slot, act, act_res, q, kcs, first, last, kofs):
                pass

        def proj_g(slot, act, act_res, q, kcs=range(8), first=True, last=True, kofs=0):
            for hf in range(NH):
                def fn(e, hf=hf):
                    r = None
                    for i, kc in enumerate(kcs):
                        r = e.matmul(psum[:, 2 * q + hf, :], ring[:, slot, kc, :],
                                     act[:, kofs + kc, hf * 512:(hf + 1) * 512],
                                     start=(first and i == 0), stop=(last and i == len(kcs) - 1))
                    return r
                P.op("pe", fn, reads=[("ring", slot)] + act_res, writes=[("b", 2 * q + hf)])
                yield

        P.dma("sp", PV[:], pv_d, writes=["PV"], group="c0")
        P.op("pool", lambda e: e.memset(identf[:], 1.0), writes=["identf"])
        P.op("pool", lambda e: e.affine_select(out=identf[:], in_=identf[:], pattern=[[-1, 128]],
                                               compare_op=ALU.is_equal, fill=0.0, base=0, channel_multiplier=1),
             reads=["identf"], writes=["identf"])
        P.op("dve", lambda e: e.tensor_copy(out=ident[:], in_=identf[:]), reads=["identf"], writes=["ident"])
        P.op("pool", lambda e: e.memset(maskf[:], 1.0), writes=["maskf"])
        P.op("pool", lambda e: e.affine_select(out=maskf[:], in_=maskf[:], pattern=[[1, 128]],
                                               compare_op=ALU.is_ge, fill=0.0, base=0, channel_multiplier=-1),
             reads=["maskf"], writes=["maskf"])
        P.op("dve", lambda e: e.memset(onesD[:], 1.0 / 1024.0), writes=["onesD"])
        P.op("dve", lambda e: e.memset(onesV[:], 1.0 / 128.0), writes=["onesV"])
        P.op("dve", lambda e: e.memset(ones1[:], 1.0), writes=["ones1"])
        P.op("dve", lambda e: e.memset(epsT[:], EPS), writes=["epsT"])
        P.op("dve", lambda e: e.memset(M01[:], 1.0), writes=["M01"])
        P.op("dve", lambda e: e.memset(M01[:].rearrange("p (c k) -> p c k", k=128)[:, :, 0:1], 0.0),
             reads=["M01"], writes=["M01"])
        P.op("dve", lambda e: e.memset(Sst[:], 0.0), writes=[("S", l_, h_) for l_ in range(L) for h_ in range(8)])
        for l in range(L):
            XS = X[:, 0:2, :].rearrange("p c t -> p (c t)")[:, 0:1024]
            P.dma("sp", XS, ws_d[:, l].rearrange("p g t -> p (g t)"), writes=[("X", 0), ("X", 1)], group="c1")
            P.op("dve", lambda e, l=l, XS=XS: e.tensor_tensor(
                out=WS[:, l], in0=XS.rearrange("p (g t) -> p g t", g=8),
                in1=maskf[:].unsqueeze(1).broadcast_to([128, 8, 128]), op=ALU.mult),
                reads=[("X", 0), ("X", 1), "maskf"], writes=["WS"])
        P.op("dve", lambda e: e.memset(BT[6][:], 0.0), writes=[("BT", 6)])
        P.op("dve", lambda e: e.memset(PT2[:], 0.0), writes=["PT2"])
        pv_lb = PV[:, 2 * L * 8:3 * L * 8].rearrange("p (l c) -> p l c", l=L)
        P.op("act", lambda e: e.activation(out=EXPL[:], in_=pv_lb, func=AF.Exp), reads=["PV"], writes=["EXPL"])

        def pl(fn, reads, writes):
            P.op("pool", fn, reads=reads, writes=writes)
        pl(lambda e: e.tensor_copy(out=small[:, 0:8], in_=EXPL[:, 0, :]), ["EXPL"], ["sm0"])
        for l in range(1, L):
            pl(lambda e, l=l: e.tensor_tensor(out=small[:, 0:8], in0=small[:, 0:8], in1=EXPL[:, l, :], op=ALU.add), ["EXPL", "sm0"], ["sm0"])
        P.op("dve", lambda e: e.reciprocal(out=small[:, 8:16], in_=small[:, 0:8]), reads=["sm0"], writes=["sm8"])
        pl(lambda e: e.memset(LB[:, 0, :], 0.0), [], [("LBl", 0)])
        for l in range(1, L):
            pl(lambda e, l=l: e.tensor_tensor(out=small[:, 16:24], in0=EXPL[:, l, :], in1=small[:, 8:16], op=ALU.mult), ["EXPL", "sm8"], ["sm16"])
            pl(lambda e, l=l: e.tensor_tensor(out=LB[:, l, :], in0=LB[:, l - 1, :], in1=small[:, 16:24], op=ALU.add), ["sm16", ("LBl", l - 1)], [("LBl", l)])
        pl(lambda e: e.tensor_scalar(out=LBm1[:], in0=LB[:], scalar1=-1.0, scalar2=None, op0=ALU.add), [("LBl", l) for l in range(L)], ["LBm1"])
        pl(lambda e: e.tensor_scalar(out=OML[:], in0=LBm1[:], scalar1=-1.0, scalar2=None, op0=ALU.mult), ["LBm1"], ["LB"])

        mixg = lambda l: PV[:, l * 8:(l + 1) * 8]
        mlpg = lambda l: PV[:, L * 8 + l * 8: L * 8 + (l + 1) * 8]
        fing = PV[:, 3 * L * 8:3 * L * 8 + 8]
        hng = lambda l: PV[:, 3 * L * 8 + 8 + l: 3 * L * 8 + 8 + l + 1]

        XR = [("X", c) for c in range(8)]
        HR = [("H", c) for c in range(8)]

        def rmsnorm(gains, to_x=False):
            q = next_pair()
            for c in range(8):
                bt = c % 2
                if bt == 0:
                    P.op("act", lambda e, c=c, bt=bt: e.activation(out=BT[bt][:], in_=X[:, c, :], func=AF.Square),
                         reads=[("X", c)], writes=[("BT", bt)])
                else:
                    P.op("pool", lambda e, c=c, bt=bt: e.tensor_tensor(out=BT[bt][:], in0=X[:, c, :], in1=X[:, c, :], op=ALU.mult),
                         reads=[("X", c)], writes=[("BT", bt)])
                for hf in range(NH):
                    P.op("pe", lambda e, c=c, bt=bt, hf=hf: e.matmul(
                        psum[:, 2 * q + hf, :], onesD[:], BT[bt][:, hf * 512:(hf + 1) * 512],
                        start=(c == 0), stop=(c == 7)),
                        reads=[("BT", bt), "onesD"], writes=[("b", 2 * q + hf)])
            RS = FT[4]
            P.op("act", lambda e: e.activation(out=RS[:], in_=pair(q), func=AF.Ln, bias=epsT[:]),
                 reads=["epsT"], writes=pair_res(q) + [("FT", 4)])
            P.op("act", lambda e: e.activation(out=RS[:], in_=RS[:], func=AF.Exp, scale=-0.5),
                 reads=[], writes=[("FT", 4)])
            for c in range(8):
                eng_ = "dve"
                if to_x:
                    P.op(eng_, lambda e, c=c: e.scalar_tensor_tensor(
                        out=X[:, c, :], in0=X[:, c, :], scalar=gains[:, c:c + 1], in1=RS[:],
                        op0=ALU.mult, op1=ALU.mult), reads=[("FT", 4), "PV"], writes=[("X", c)])
                else:
                    P.op(eng_, lambda e, c=c: e.scalar_tensor_tensor(
                        out=H[:, c, :], in0=X[:, c, :], scalar=gains[:, c:c + 1], in1=RS[:],
                        op0=ALU.mult, op1=ALU.mult), reads=[("FT", 4), "PV", ("X", c)], writes=[("H", c)])

        def to_token_major(src_bt, c, reg=16):
            for b0 in range(0, NB, 8):
                nb = min(8, NB - b0)
                sbk = next_single()
                pT = bank(sbk).bitcast(BF16)

                def fn(e, b0=b0, nb=nb, pT=pT):
                    r = None
                    for j in range(nb):
                        r = e.transpose(out=pT[:, j * 128:(j + 1) * 128],
                                        in_=BT[src_bt][:, (b0 + j) * 128:(b0 + j + 1) * 128], identity=ident[:])
                    return r
                P.op("pe", fn, reads=[("BT", src_bt), "ident"], writes=[("b", sbk)])
                res = []
                for j in range(nb):
                    res += vt_res(b0 + j, reg)

                def ev(e, b0=b0, nb=nb, pT=pT):
                    dst = VTflats[reg][:, b0 * 1024:(b0 + nb) * 1024].rearrange("p (j f) -> p j f", f=1024)[:, :, c * 128:(c + 1) * 128]
                    return e.tensor_copy(out=dst, in_=pT[:, 0:nb * 128].rearrange("p (j f) -> p j f", f=128))
                P.op("dve", ev, reads=[], writes=[("b", sbk)] + sorted(set(res)))

        def dump(src_fn):
            for c in range(8):
                P.op("dve", lambda e, c=c: e.tensor_copy(out=X[:, c, :], in_=src_fn(c)),
                     reads=HR + [("R", i) for i in range(24)], writes=[("X", c)])
            raise _Stop()

        def layer(l):
            base = {"pi": 0}

            def panel(key):
                pi = base["pi"]
                base["pi"] += 1
                if len(panel_order) < NPANEL:
                    panel_order.append(key)
                else:
                    assert panel_order[pi] == key, (pi, key, panel_order[pi])
                return next_panel(l, pi)

            P.dma("pool", GBC[:], sg_d[l].partition_broadcast(128), writes=["GBC"], group="gbc")
            P.dma("pool", BBC[:], sb_d[l].partition_broadcast(128), writes=["BBC"], group="gbc")
            P.dma("sp", BSR[:], bs_d[l:l + 1, :], writes=["BSR"], group="bsr")

            rmsnorm(mixg(l))

            if debug == "norm":
                dump(lambda c: H[:, c, :])
            for c in range(8):
                s_ = panel(('v', c))
                q = next_pair()
                proj(s_, H, HR, q)
                bt = c % 2
                P.op("act", lambda e, q=q, bt=bt: e.activation(out=BT[bt][:], in_=pair(q), func=AF.Gelu_apprx_tanh),
                     reads=[], writes=pair_res(q) + [("BT", bt)])
                if c > 0:
                    to_token_major((c - 1) % 2, c - 1, reg=8)
            to_token_major(7 % 2, 7, reg=8)
            def zacc(e):
                e.memset(small[:, 24:24 + NB], 0.0)
                return e.memset(small[:, 40:40 + NB], 0.0)
            P.op("dve", zacc, reads=[], writes=[("sm", 24 + b) for b in range(NB)] + [("sm", 40 + b) for b in range(NB)] + ["lnstat", "lnstat2", "lnstat3", "ln_mean", "ln_m2", "ln_e2"])
            for c in range(8):
                s_ = panel(('i', c))
                q = next_pair()
                proj(s_, H, HR, q)
                bt = c % 2
                P.op("act", lambda e, q=q, bt=bt: e.activation(out=BT[bt][:], in_=pair(q), func=AF.Copy),
                     reads=[], writes=pair_res(q) + [("BT", bt)])
                if c > 0:
                    to_token_major((c - 1) % 2, c - 1, reg=16)
                for blk in range(c * NB // 8, (c + 1) * NB // 8):
                    P.op("act", lambda e, blk=blk: e.activation(out=JK[:], in_=VT(blk, reg=8), func=AF.Copy,
                                                              accum_out=small[:, 24 + blk:25 + blk]),
                         reads=vt_res(blk, 8), writes=[("BT", 5), ("sm", 24 + blk)])
                    P.op("act", lambda e, blk=blk: e.activation(out=JK[:], in_=VT(blk, reg=8), func=AF.Square,
                                                              accum_out=small[:, 40 + blk:41 + blk]),
                         reads=vt_res(blk, 8), writes=[("BT", 5), ("sm", 40 + blk)])

            to_token_major(7 % 2, 7, reg=16)
            smr = [("sm", 24 + b) for b in range(NB)] + [("sm", 40 + b) for b in range(NB)]
            P.op("pool", lambda e: e.tensor_scalar(out=small[:, 24:24 + NB], in0=small[:, 24:24 + NB], scalar1=1.0 / 1024, scalar2=None, op0=ALU.mult),
                 reads=smr, writes=["ln_mean"])
            P.op("pool", lambda e: e.tensor_tensor(out=small[:, 56:56 + NB], in0=small[:, 24:24 + NB], in1=small[:, 24:24 + NB], op=ALU.mult),
                 reads=["ln_mean"], writes=["ln_m2"])
            P.op("pool", lambda e: e.tensor_scalar(out=small[:, 40:40 + NB], in0=small[:, 40:40 + NB], scalar1=1.0 / 1024, scalar2=None, op0=ALU.mult),
                 reads=smr, writes=["ln_e2"])
            P.op("pool", lambda e: e.tensor_tensor(out=small[:, 40:40 + NB], in0=small[:, 40:40 + NB], in1=small[:, 56:56 + NB], op=ALU.subtract),
                 reads=["ln_e2", "ln_m2"], writes=["ln_e2"])
            P.op("pool", lambda e: e.tensor_scalar(out=small[:, 40:40 + NB], in0=small[:, 40:40 + NB], scalar1=EPS, scalar2=None, op0=ALU.add),
                 reads=["ln_e2"], writes=["ln_e2", "lnstat"])
            P.op("act", lambda e: e.activation(out=small[:, 40:40 + NB], in_=small[:, 40:40 + NB], func=AF.Ln),
                 reads=["lnstat"], writes=["lnstat2"])
            P.op("act", lambda e: e.activation(out=small[:, 40:40 + NB], in_=small[:, 40:40 + NB], func=AF.Exp, scale=-0.5),
                 reads=["lnstat2"], writes=["lnstat3"])
            for blk in range(NB):
                def lnap(e, blk=blk):
                    e.tensor_scalar(out=VT(blk, reg=8), in0=VT(blk, reg=8), scalar1=small[:, 24 + blk:25 + blk],
                                    scalar2=small[:, 40 + blk:41 + blk], op0=ALU.subtract, op1=ALU.mult)
                    e.tensor_tensor(out=VT(blk, reg=8), in0=VT(blk, reg=8), in1=GBC[:], op=ALU.mult)
                    return e.tensor_tensor(out=VT(blk, reg=8), in0=VT(blk, reg=8), in1=BBC[:], op=ALU.add)
                P.op("dve", lnap, reads=["lnstat3", "ln_mean", "GBC", "BBC"], writes=vt_res(blk, 8))

            def phaseU():
                for g in range(8):
                    s_ = panel(('u', g))
                    q = next_pair()
                    for _ in proj_g(s_, H, HR, q):
                        yield
                    bt = g % 2
                    P.op("act", lambda e, q=q, bt=bt: e.activation(out=BT[bt][:], in_=pair(q), func=AF.Gelu_apprx_tanh),
                         reads=[], writes=pair_res(q) + [("BT", bt)])
                    yield
                    qm = next_pair()
                    for hf in range(NH):
                        def mix(e, hf=hf, g=g, qm=qm):
                            r = None
                            for j in range(4):
                                blk = hf * 4 + j
                                e.matmul(psum[:, 2 * qm + hf, j * 128:(j + 1) * 128], ones1[0:1, :],
                                         BSR[0:1, g * 128:(g + 1) * 128], start=(j == 0), stop=False)
                                r = e.matmul(psum[:, 2 * qm + hf, j * 128:(j + 1) * 128],
                                             VT(blk, g * 128, (g + 1) * 128, reg=8), WS[:, l, g, :],
                                             start=False, stop=(j == 3))
                            return r
                        res = []
                        for j in range(4):
                            res += vt_res(hf * 4 + j, 8)
                        P.op("pe", mix, reads=sorted(set(res)) + ["WS", "BSR", "ones1"], writes=[("b", 2 * qm + hf)])
                        yield
                    P.op("dve", lambda e, g=g, qm=qm, bt=bt: e.tensor_tensor(out=A_(g), in0=pair(qm), in1=BT[bt][:], op=ALU.mult),
                         reads=[("BT", bt)], writes=pair_res(qm) + [("R", g)])
                    yield

            SG, LF, BC, DD, RS = FT[0], FT[1], FT[2], FT[3], FT[4]
            KT, OSQ = BT[2], BT[5]
            QTs, GSs, PTss, KTTs = [BT[3], BT[0]], [BT[4], BT[1]], [BT[6], PT2], [BT[7], KTT2]
            QTr, GSr, PTr, KTr = [("BT", 3), ("BT", 0)], [("BT", 4), ("BT", 1)], [("BT", 6), "PT2"], [("BT", 7), "KTT2"]
            EBr = [("EB", 0), ("EB", 1)]

            def prepA(hh):
                p = hh % 2
                QT, GS, PTs, KTT = QTs[p], GSs[p], PTss[p], KTTs[p]
                EB1 = EB2[:, p, 0:NB]
                EBB = EB2[:, p, NB:2 * NB]
                sf_ = panel(('f', hh))
                qf = next_pair()
                for _ in proj_g(sf_, H, HR, qf):
                    yield
                P.op("act", lambda e, qf=qf: e.activation(out=SG[:], in_=pair(qf), func=AF.Sigmoid),
                     reads=[], writes=pair_res(qf) + [("FT", 0)])
                yield
                P.op("act", lambda e, hh=hh: e.activation(out=LF[:], in_=SG[:], func=AF.Ln,
                                                          scale=OML[:, l, hh:hh + 1], bias=LB[:, l, hh:hh + 1]),
                     reads=[("FT", 0), "LB"], writes=[("FT", 1)])
                yield
                P.op("dve", lambda e: e.tensor_tensor_scan(out=BC[:], data0=M01[:], data1=LF[:], initial=0.0,
                                                           op0=ALU.mult, op1=ALU.add),
                     reads=[("FT", 1), "M01"], writes=[("FT", 2)])
                yield
                BCv = BC[:].rearrange("p (c k) -> p c k", k=128)
                P.op("dve", lambda e, BCv=BCv: e.tensor_tensor(
                    out=DD[:].rearrange("p (c k) -> p c k", k=128), in0=BCv[:, :, 63:64].broadcast_to([128, NB, 128]),
                    in1=BCv, op=ALU.subtract), reads=[("FT", 2)], writes=[("FT", 3)])
                yield
                P.op("act", lambda e, BCv=BCv, EB1=EB1: e.activation(out=EB1.unsqueeze(2), in_=BCv[:, :, 63:64], func=AF.Exp),
                     reads=[("FT", 2)], writes=[EBr[p]])
                P.op("act", lambda e, BCv=BCv, EBB=EBB: e.activation(out=EBB.unsqueeze(2), in_=BCv[:, :, 127:128], func=AF.Exp),
                     reads=[("FT", 2)], writes=[EBr[p]])
                P.op("dve", lambda e, BCv=BCv: e.tensor_tensor(
                    out=LF[:].rearrange("p (c k) -> p c k", k=128), in0=BCv[:, :, 127:128].broadcast_to([128, NB, 128]),
                    in1=BCv, op=ALU.subtract), reads=[("FT", 2)], writes=[("FT", 1)])
                yield
                P.op("dve", lambda e, hh=hh: e.tensor_scalar(out=SG[:], in0=SG[:], scalar1=-1.0, scalar2=LBm1[:, l, hh:hh + 1],
                                                             op0=ALU.add, op1=ALU.mult),
                     reads=["LB"], writes=[("FT", 0)])
                yield
                P.op("act", lambda e: e.activation(out=LF[:], in_=LF[:], func=AF.Exp),
                     reads=[], writes=[("FT", 1)])
                yield
                P.op("dve", lambda e: e.tensor_tensor(out=KT[:], in0=SG[:], in1=LF[:], op=ALU.mult),
                     reads=[("FT", 0), ("FT", 1)], writes=[("BT", 2)])
                yield
                sbk = next_single()
                pT = bank(sbk).bitcast(BF16)

                def ktr(e, pT=pT):
                    r = None
                    for j in range(NB):
                        r = e.transpose(out=pT[:, j * 128:(j + 1) * 128],
                                        in_=KT[:, j * 128:(j + 1) * 128], identity=ident[:])
                    return r
                P.op("pe", ktr, reads=[("BT", 2), "ident"], writes=[("b", sbk)])
                P.op("dve", lambda e, pT=pT, KTT=KTT: e.tensor_copy(out=KTT[:, 0:NB * 128], in_=pT[:, 0:NB * 128]),
                     reads=[], writes=[("b", sbk), KTr[p]])
                yield
                P.op("act", lambda e: e.activation(out=LF[:], in_=DD[:], func=AF.Exp),
                     reads=[("FT", 3)], writes=[("FT", 1)])
                yield
                P.op("dve", lambda e: e.tensor_tensor(out=KT[:], in0=SG[:], in1=LF[:], op=ALU.mult),
                     reads=[("FT", 0), ("FT", 1)], writes=[("BT", 2)])
                yield
                P.op("act", lambda e: e.activation(out=DD[:], in_=DD[:], func=AF.Exp, scale=-1.0),
                     reads=[], writes=[("FT", 3)])
                yield
                sq_ = panel(('q', hh))
                qq = next_pair()
                for _ in proj_g(sq_, H, HR, qq):
                    yield
                P.op("act", lambda e, qq=qq: e.activation(out=BC[:], in_=pair(qq), func=AF.Silu),
                     reads=[EBr[p]], writes=pair_res(qq) + [("FT", 2)])
                yield
                P.op("dve", lambda e, QT=QT: e.tensor_tensor(out=QT[:], in0=BC[:], in1=DD[:], op=ALU.mult),
                     reads=[("FT", 2), ("FT", 3)], writes=[QTr[p]])
                yield
                sg_ = panel(('g', hh))
                qg = next_pair()
                for _ in proj_g(sg_, H, HR, qg):
                    yield
                P.op("act", lambda e, qg=qg, GS=GS: e.activation(out=GS[:], in_=pair(qg), func=AF.Silu),
                     reads=[], writes=pair_res(qg) + [GSr[p]])
                yield
                qs = next_pair()
                for hf in range(NH):
                    def sc(e, hf=hf, qs=qs, QT=QT):
                        r = None
                        for j in range(4):
                            blk = hf * 4 + j
                            e.matmul(psum[:, 2 * qs + hf, j * 128 + 64:(j + 1) * 128],
                                     KT[:, blk * 128:(blk + 1) * 128], QT[:, blk * 128 + 64:(blk + 1) * 128],
                                     start=(j == 0), stop=False)
                            r = e.matmul(psum[0:64, 2 * qs + hf, j * 128:j * 128 + 64],
                                         KT[:, blk * 128:blk * 128 + 64], QT[:, blk * 128:blk * 128 + 64],
                                         start=False, stop=(j == 3))
                        return r
                    P.op("pe", sc, reads=[("BT", 2), QTr[p]], writes=[("b", 2 * qs + hf)])
                    yield

                def mk(e, qs=qs, PTs=PTs):
                    pv = pair(qs).rearrange("p (j t) -> p j t", t=128)
                    tv = PTs[:].rearrange("p (j t) -> p j t", t=128)
                    e.tensor_tensor(out=tv[0:64], in0=pv[0:64], in1=maskf[0:64, :].unsqueeze(1).broadcast_to([64, NB, 128]),
                                    op=ALU.mult)
                    return e.tensor_tensor(out=tv[64:128, :, 64:128], in0=pv[64:128, :, 64:128],
                                           in1=maskf[64:128, 64:128].unsqueeze(1).broadcast_to([64, NB, 64]), op=ALU.mult)
                P.op("dve", mk, reads=["maskf"], writes=pair_res(qs) + [PTr[p]])
                yield

            def chainB(hh):
                if OLD_CHAIN:
                    yield from chainB_old(hh)
                else:
                    yield from chainB_new(hh)

            def chainB_old(hh):
                p = hh % 2
                QT, GS, PTs, KTT = QTs[p], GSs[p], PTss[p], KTTs[p]
                Sv = Sst[:, l, hh, :]
                qo = 2
                for blk in range(NB):
                    hf, j = blk // 4, blk % 4
                    par = blk % 2
                    vals = lambda p0, p1, blk=blk, hh=hh: VT(blk, hh * 128, (hh + 1) * 128, p0, p1)
                    P.op("pe", lambda e, blk=blk, hf=hf, j=j, vals=vals, PTs=PTs: e.matmul(
                        psum[:, 2 * qo + hf, j * 128:(j + 1) * 128], vals(0, 128), PTs[:, blk * 128:(blk + 1) * 128],
                        start=(j == 0), stop=False),
                        reads=vt_res(blk) + [PTr[p]], writes=[("b", 2 * qo + hf)])
                    P.op("act", lambda e, blk=blk, par=par, Sv=Sv, p=p: e.activation(out=SRall[:, par, :], in_=Sv, func=AF.Copy, scale=EB2[:, p, blk:blk + 1]),
                         reads=[EBr[p], ("S", l, hh)], writes=[("SR", par)])
                    P.op("pe", lambda e, blk=blk, hf=hf, j=j, par=par, QT=QT: e.matmul(
                        psum[:, 2 * qo + hf, j * 128:(j + 1) * 128], SRall[:, par, :],
                        QT[:, blk * 128:(blk + 1) * 128], start=False, stop=(j == 3)),
                        reads=[("SR", par), QTr[p]], writes=[("b", 2 * qo + hf)])
                    cb = next_chain()
                    P.op("pe", lambda e, blk=blk, cb=cb, vals=vals, KTT=KTT: e.matmul(
                        psum[:, cb, 0:128], KTT[:, blk * 128:(blk + 1) * 128], vals(0, 128),
                        start=True, stop=True),
                        reads=vt_res(blk) + [KTr[p]], writes=[("b", cb)])
                    P.op("dve", lambda e, cb=cb, blk=blk, Sv=Sv, p=p: e.scalar_tensor_tensor(
                        out=Sv, in0=Sv, scalar=EB2[:, p, NB + blk:NB + blk + 1], in1=psum[:, cb, 0:128],
                        op0=ALU.mult, op1=ALU.add),
                        reads=[EBr[p]], writes=[("b", cb), ("S", l, hh)])
                    yield
                yield from chain_end(hh)

            def chainB_new(hh):
                p = hh % 2
                QT, GS, PTs, KTT = QTs[p], GSs[p], PTss[p], KTTs[p]
                Sv = Sst[:, l, hh, :]
                qo = 2
                vals = lambda blk, p0, p1, hh=hh: VT(blk, hh * 128, (hh + 1) * 128, p0, p1)
                P.op("dve", lambda e, Sv=Sv, p=p: e.tensor_scalar(out=SRall[:, 0, :], in0=Sv, scalar1=EB2[:, p, 0:1], scalar2=None, op0=ALU.mult),
                     reads=[EBr[p], ("S", l, hh)], writes=[("SRs", 0)])
                yield
                for hf in range(NH):
                    def intra(e, hf=hf, PTs=PTs, vals=vals):
                        r = None
                        for j in range(4):
                            blk = hf * 4 + j
                            r = e.matmul(psum[:, 2 * qo + hf, j * 128:(j + 1) * 128], vals(blk, 0, 128),
                                         PTs[:, blk * 128:(blk + 1) * 128], start=(j == 0), stop=False)
                        return r
                    res = []
                    for j in range(4):
                        res += vt_res(hf * 4 + j)
                    P.op("pe", intra, reads=sorted(set(res)) + [PTr[p]], writes=[("b", 2 * qo + hf)])
                    yield
                for g in range(NCK // 4):
                    cb = 6 + g % 2

                    def ugrp(e, g=g, cb=cb, KTT=KTT, vals=vals):
                        r = None
                        for i in range(4):
                            c = 4 * g + i
                            blk, ck = c // 2, c % 2
                            p0 = ck * 64
                            r = e.matmul(psum[:, cb, i * 128:(i + 1) * 128], KTT[p0:p0 + 64, blk * 128:(blk + 1) * 128],
                                         vals(blk, p0, p0 + 64), start=True, stop=True)
                        return r
                    P.op("pe", ugrp, reads=vt_res(2 * g) + vt_res(2 * g + 1) + [KTr[p]], writes=[("b", cb)])
                    yield

                    def scan(e, g=g, cb=cb, Sv=Sv, p=p):
                        r = None
                        for i in range(4):
                            c = 4 * g + i
                            for hv in range(2):
                                r = e.scalar_tensor_tensor(out=Sv[:, hv * 64:hv * 64 + 64], in0=Sv[:, hv * 64:hv * 64 + 64],
                                                           scalar=EB2[:, p, c:c + 1],
                                                           in1=psum[:, cb, i * 128 + hv * 64:i * 128 + hv * 64 + 64],
                                                           op0=ALU.mult, op1=ALU.add)
                            if c + 1 < NCK:
                                for hv in range(2):
                                    r = e.tensor_scalar(out=SRall[:, c + 1, hv * 64:hv * 64 + 64], in0=Sv[:, hv * 64:hv * 64 + 64],
                                                        scalar1=EB2[:, p, c + 1:c + 2], scalar2=None, op0=ALU.mult)
                        return r
                    P.op("dve", scan, reads=[EBr[p]], writes=[("b", cb), ("S", l, hh), ("SRs", g + 1)])
                    yield

                    def inter(e, g=g, QT=QT):
                        r = None
                        for i in range(4):
                            c = 4 * g + i
                            blk, ck = c // 2, c % 2
                            hf, j = blk // 4, blk % 4
                            r = e.matmul(psum[:, 2 * qo + hf, j * 128 + ck * 64:j * 128 + ck * 64 + 64], SRall[:, c, :],
                                         QT[:, blk * 128 + ck * 64:blk * 128 + ck * 64 + 64], start=False, stop=(i == 3))
                        return r
                    P.op("pe", inter, reads=[("SRs", g), ("SRs", g + 1), QTr[p]], writes=[("b", 2 * qo + (2 * g) // 4)])
                    yield
                yield from chain_end(hh)

            def chain_end(hh):
                p = hh % 2
                GS = GSs[p]
                qo = 2
                P.op("act", lambda e: e.activation(out=OSQ[:], in_=pair(qo), func=AF.Square),
                     reads=[], writes=pair_res(qo) + [("BT", 5)])
                yield
                qn = next_pair()
                for hf in range(NH):
                    P.op("pe", lambda e, hf=hf, qn=qn: e.matmul(psum[:, 2 * qn + hf, :], onesV[:],
                                                               OSQ[:, hf * 512:(hf + 1) * 512], start=True, stop=True),
                         reads=[("BT", 5), "onesV"], writes=[("b", 2 * qn + hf)])
                yield
                P.op("act", lambda e, qn=qn: e.activation(out=RS[:], in_=pair(qn), func=AF.Ln, bias=epsT[:]),
                     reads=["epsT"], writes=pair_res(qn) + [("FT", 4)])
                P.op("act", lambda e: e.activation(out=RS[:], in_=RS[:], func=AF.Exp, scale=-0.5),
                     reads=[], writes=[("FT", 4)])
                yield
                P.op("dve", lambda e: e.scalar_tensor_tensor(
                    out=RS[:], in0=pair(qo), scalar=hng(l), in1=RS[:], op0=ALU.mult, op1=ALU.mult),
                    reads=["PV"], writes=pair_res(qo) + [("FT", 4)])
                P.op("dve", lambda e, hh=hh, GS=GS: e.tensor_tensor(out=Bb_(hh), in0=RS[:], in1=GS[:], op=ALU.mult),
                     reads=[("FT", 4), GSr[p]], writes=[("R", 8 + hh)])
                yield

            gU = phaseU()
            gA0 = prepA(0)
            for _ in gU:
                next(gA0, None)
            for _ in gA0:
                pass
            qstate["hmode"] = True
            for hh in range(8):
                gB = chainB(hh)
                gA = prepA(hh + 1) if hh < 7 else iter(())
                if INTERLEAVE:
                    for _ in gB:
                        next(gA, None)
                        next(gA, None)
                    for _ in gA:
                        pass
                else:
                    for _ in gB:
                        pass
                    for _ in gA:
                        pass
            qstate["hmode"] = False

            if debug == "b":
                dump(lambda c: R[:, 8 + c, :])
            AR = [("R", g) for g in range(8)]
            BR = [("R", 8 + g) for g in range(8)]
            for m in range(8):
                sga = panel(('ga', m))
                q1 = next_pair()
                proj(sga, H, HR, q1)
                P.op("act", lambda e, q1=q1: e.activation(out=FT[0][:], in_=pair(q1), func=AF.Sigmoid),
                     reads=[], writes=pair_res(q1) + [("FT", 0)])
                sgb = panel(('gb', m))
                q2 = next_pair()
                proj(sgb, H, HR, q2)
                P.op("act", lambda e, q2=q2: e.activation(out=FT[1][:], in_=pair(q2), func=AF.Sigmoid),
                     reads=[], writes=pair_res(q2) + [("FT", 1)])
                sa = panel(('pa', m))
                q3 = next_pair()
                proj(sa, R, AR, q3)
                P.op("dve", lambda e, q3=q3: e.tensor_tensor(out=FT[0][:], in0=pair(q3), in1=FT[0][:], op=ALU.mult),
                     reads=[], writes=pair_res(q3) + [("FT", 0)])
                sbp = panel(('pb', m))
                q4 = next_pair()
                proj(sbp, R, BR, q4, kofs=8)
                P.op("dve", lambda e, q4=q4: e.tensor_tensor(out=FT[1][:], in0=pair(q4), in1=FT[1][:], op=ALU.mult),
                     reads=[], writes=pair_res(q4) + [("FT", 1)])
                P.op("dve", lambda e, m=m: e.tensor_tensor(out=MG_(m), in0=FT[0][:], in1=FT[1][:], op=ALU.add),
                     reads=[("FT", 0), ("FT", 1)], writes=[("R", 16 + m)])
            if debug == "merged":
                dump(lambda c: R[:, 16 + c, :])
            MR = [("R", 16 + g) for g in range(8)]
            for m in range(8):
                s_ = panel(('o', m))
                q = next_pair()
                proj(s_, R, MR, q, kofs=16)
                P.op("dve", lambda e, q=q, m=m: e.tensor_tensor(out=X[:, m, :], in0=pair(q), in1=X[:, m, :], op=ALU.add),
                     reads=[], writes=pair_res(q) + [("X", m)])

            if debug == "mixer":
                raise _Stop()
            rmsnorm(mlpg(l))
            for half in range(2):
                for j in range(16):
                    s_ = panel(('up', half * 16 + j))
                    q = next_pair()
                    proj(s_, H, HR, q)
                    P.op("act", lambda e, q=q, j=j: e.activation(out=HID_(j), in_=pair(q), func=AF.Relu),
                         reads=[], writes=pair_res(q) + [("R", j)])
                    P.op("dve", lambda e, j=j: e.tensor_tensor(out=HID_(j), in0=HID_(j), in1=HID_(j), op=ALU.mult),
                         reads=[], writes=[("R", j)])
                HIDR = [("R", j) for j in range(16)]
                for m in range(8):
                    q = next_pair()
                    s0 = panel(('dn', half, 0, m))
                    proj(s0, R, HIDR, q, first=True, last=False, kofs=0)
                    s1 = panel(('dn', half, 1, m))
                    proj(s1, R, HIDR, q, first=False, last=True, kofs=8)
                    P.op("dve", lambda e, q=q, m=m: e.tensor_tensor(out=X[:, m, :], in0=pair(q), in1=X[:, m, :], op=ALU.add),
                         reads=[], writes=pair_res(q) + [("X", m)])
            assert base["pi"] == NPANEL

        last = None
        for t in range(NT):
            for c in range(8):
                P.dma("sp", X[:, c, :], x_d[:, c, t * T:(t + 1) * T], writes=[("X", c)], group=("xin", c))
            try:
                for l in range(L):
                    layer(l)
                rmsnorm(fing, to_x=True)
            except _Stop:
                pass
            lasts = []
            for c in range(8):
                lasts.append(P.dma("sp", y_d[:, c, t * T:(t + 1) * T], X[:, c, :], reads=[("X", c)], group=("yout", c)))
        P.finalize(final_waits=lasts)
    nc._panel_order = list(panel_order)
    return nc


def _panelize(w):
    k, n = w.shape
    assert k == 1024
    return np.ascontiguousarray(w.reshape(8, 128, n // 128, 128).transpose(2, 1, 0, 3))


def prep_weights(order, w_in, w_branch_a, w_branch_b, w_out, w_mlp_up, w_mlp_down):
    L = w_in.shape[0]
    out = np.empty((L, NPANEL, 128, 8, 128), np.float32)
    base = {'u': 0, 'v': 8, 'q': 16, 'f': 24, 'i': 32, 'g': 40, 'ga': 48, 'gb': 56}
    assert len(order) == NPANEL and len(set(order)) == NPANEL
    for l in range(L):
        pin = _panelize(w_in[l])
        pa = _panelize(w_branch_a[l])
        pb = _panelize(w_branch_b[l])
        po = _panelize(w_out[l])
        pu = _panelize(w_mlp_up[l])
        wd = w_mlp_down[l]
        pd = [_panelize(wd[i * 1024:(i + 1) * 1024]) for i in range(4)]
        for i, key in enumerate(order):
            k0 = key[0]
            if k0 in base:
                out[l, i] = pin[base[k0] + key[1]]
            elif k0 == 'pa':
                out[l, i] = pa[key[1]]
            elif k0 == 'pb':
                out[l, i] = pb[key[1]]
            elif k0 == 'o':
                out[l, i] = po[key[1]]
            elif k0 == 'up':
                out[l, i] = pu[key[1]]
            elif k0 == 'dn':
                out[l, i] = pd[key[1] * 2 + key[2]][key[3]]
            else:
                raise KeyError(key)
    return out


def _pp(v):
    lead = v.shape[:-1]
    a = v.reshape(*lead, 8, 128)
    return np.moveaxis(a, -1, 0)


_CACHE = {}


def kernel(x, mix_norm_g, w_in, sgu_norm_g, sgu_norm_b, w_spatial, b_spatial, lower_bounds,
           hgrn_norm_g, w_branch_a, w_branch_b, w_out, mlp_norm_g, w_mlp_up, w_mlp_down,
           final_norm_g, T=1024, debug=None):
    x = np.asarray(x, np.float32)
    B, S, _ = x.shape
    L = int(np.asarray(w_in).shape[0])
    f = lambda a: np.asarray(a, np.float32)
    ws = np.ascontiguousarray(f(w_spatial).transpose(3, 0, 1, 2))
    bs = np.ascontiguousarray(f(b_spatial).reshape(L, 1024))
    pvec = np.concatenate([
        _pp(f(mix_norm_g)).reshape(128, L * 8),
        _pp(f(mlp_norm_g)).reshape(128, L * 8),
        _pp(f(lower_bounds)).reshape(128, L * 8),
        _pp(f(final_norm_g)).reshape(128, 8),
        np.ascontiguousarray(f(hgrn_norm_g).T),
    ], axis=1).astype(np.float32)
    pvec = np.ascontiguousarray(pvec)
    key = (S, L, T, debug)
    if key not in _CACHE:
        _CACHE[key] = build_program(S, L, T, debug=debug)
    nc = _CACHE[key]
    wp = prep_weights(nc._panel_order, f(w_in), f(w_branch_a), f(w_branch_b), f(w_out), f(w_mlp_up), f(w_mlp_down))
    in_maps = []
    for b in range(B):
        xb = np.ascontiguousarray(x[b].T.reshape(8, 128, S).transpose(1, 0, 2))
        in_maps.append({"x": xb, "wp": wp, "ws": ws, "bs": bs, "sgu_g": f(sgu_norm_g), "sgu_b": f(sgu_norm_b),
                        "pvec": pvec})
    res = run_bass_kernel_spmd(nc, in_maps, core_ids=list(range(B)))
    out = np.empty((B, S, D), np.float32)
    for b in range(B):
        yb = res.results[b]["y"]
        out[b] = yb.transpose(2, 1, 0).reshape(S, D)
    return out
```
